# Optimizing a Trainium2 kernel written in Bass

```python
import math
import jax, jax.numpy as jnp
from jax import lax
import numpy as np

D_MODEL = 2048
BATCH = 4
SEQ = 4096
DEPTH = 2

GRID_W = 64
CTX_LEN = 256
HEAD_DIM = 64

SSD_HEADS = 8
SSD_HEAD_DIM = 64
SSD_D_INNER = SSD_HEADS * SSD_HEAD_DIM
SSD_STATE = 128
SSD_GROUPS = 2
SSD_CONV = 5
SSD_CHUNK = 128
SSD_XBC = SSD_D_INNER + 2 * SSD_GROUPS * SSD_STATE

NA_HEADS = 8
NA_WIN_ROWS = 8
NA_WIN_COLS = 16
NA_COL_BLOCK = 16
NA_WIDTH = NA_HEADS * HEAD_DIM

SWA_HEADS = 8
SWA_KV_HEADS = 2
SWA_WINDOW = 128
SWA_BLOCK = 128
SWA_WIDTH = SWA_HEADS * HEAD_DIM

FN_GROUPS = 4
FN_GROUP_DIM = 128
FN_WIDTH = FN_GROUPS * FN_GROUP_DIM

N_BRANCH = 4
BRANCH_WIDTH = 512
D_FF = 5632
FFN_CONV = 3
ROPE_BASE = 10000.0
LN_EPS = 1e-6
ALPHA = (2.0 * DEPTH) ** 0.25
BETA = (8.0 * DEPTH) ** -0.25

IN_SIZES = (SSD_D_INNER, SSD_XBC, 2 * SSD_HEADS, 3 * NA_WIDTH, SWA_WIDTH,
            2 * SWA_KV_HEADS * HEAD_DIM, FN_WIDTH, N_BRANCH * D_MODEL)
IN_COLS = sum(IN_SIZES)

kernel_name = "hybrid_dit_ssd_natten_swa_fnet"


def layer_norm(x, g=None, b=None):
    xf = x.astype(jnp.float32)
    mu = xf.mean(-1, keepdims=True)
    var = jnp.square(xf - mu).mean(-1, keepdims=True)
    y = (xf - mu) * lax.rsqrt(var + LN_EPS)
    if g is not None:
        y = y * g.astype(jnp.float32) + b.astype(jnp.float32)
    return y.astype(x.dtype)


def rms_norm(x, g):
    xf = x.astype(jnp.float32)
    y = xf * lax.rsqrt(jnp.mean(jnp.square(xf), -1, keepdims=True) + LN_EPS) * g.astype(jnp.float32)
    return y.astype(x.dtype)


def dwconv(x, w, b):
    k = w.shape[0]
    y = lax.conv_general_dilated(x, w[:, None, :].astype(x.dtype), window_strides=(1,),
                                 padding=[(k // 2, k // 2)],
                                 dimension_numbers=('NWC', 'WIO', 'NWC'),
                                 feature_group_count=x.shape[-1])
    return y + b.astype(x.dtype)


def grid_rope(n_tokens):
    t = jnp.arange(n_tokens)
    rows = (t // GRID_W).astype(jnp.float32)
    cols = (t % GRID_W).astype(jnp.float32)
    n_freq = HEAD_DIM // 4
    inv = ROPE_BASE ** (-jnp.arange(n_freq, dtype=jnp.float32) / n_freq)
    ang = jnp.stack([rows[:, None] * inv, cols[:, None] * inv], axis=1)
    return jnp.cos(ang), jnp.sin(ang)


def apply_rope(x, cos, sin):
    xa = x.reshape(*x.shape[:-1], 2, 2, HEAD_DIM // 4)
    x1, x2 = xa[..., 0, :], xa[..., 1, :]
    cs, sn = cos[:, None], sin[:, None]
    out = jnp.stack([x1 * cs - x2 * sn, x2 * cs + x1 * sn], axis=-2)
    return out.reshape(x.shape).astype(x.dtype)


def _heads(t):
    return t.reshape(*t.shape[:2], -1, HEAD_DIM)


def ctx_attention(qc, kc, vc, sink=None):
    nb, n, hq, d = qc.shape
    hkv = kc.shape[2]
    g = hq // hkv
    qg = qc.reshape(nb, n, hkv, g, d)
    s = jnp.einsum('bqhgd,bkhd->bhgqk', qg, kc).astype(jnp.float32) * (d ** -0.5)
    if sink is not None:
        s_sink = jnp.broadcast_to(sink.astype(jnp.float32).reshape(hkv, g)[None, :, :, None, None],
                                  s.shape[:-1] + (1,))
        p = jax.nn.softmax(jnp.concatenate([s, s_sink], -1), axis=-1)[..., :n]
    else:
        p = jax.nn.softmax(s, axis=-1)
    o = jnp.einsum('bhgqk,bkhd->bqhgd', p.astype(vc.dtype), vc)
    return o.reshape(nb, n, hq * d)


def ssd_states(x, dt, a, bm, h0):
    nb, n = x.shape[:2]
    nc, r = n // SSD_CHUNK, SSD_HEADS // SSD_GROUPS
    xq = x.reshape(nb, nc, SSD_CHUNK, SSD_GROUPS, r, SSD_HEAD_DIM)
    dtq = dt.reshape(nb, nc, SSD_CHUNK, SSD_GROUPS, r)
    bq = bm.reshape(nb, nc, SSD_CHUNK, SSD_GROUPS, SSD_STATE)
    a_cum = jnp.cumsum(dtq * a.reshape(SSD_GROUPS, r), axis=2)
    w_end = jnp.exp(a_cum[:, :, -1:] - a_cum) * dtq
    chunk_st = jnp.einsum('bcjgn,bcjgr,bcjgrp->bcgrpn', bq, w_end, xq)
    chunk_dec = jnp.exp(a_cum[:, :, -1])

    def step(h, inp):
        st, dec = inp
        return dec[..., None, None] * h + st, h

    h_fin, h_start = lax.scan(step, h0, (jnp.moveaxis(chunk_st, 1, 0), jnp.moveaxis(chunk_dec, 1, 0)))
    return a_cum, jnp.moveaxis(h_start, 0, 1), h_fin


def ssd_outputs(x, dt, cm, bm, a_cum, h_start):
    nb, n = x.shape[:2]
    nc, r = n // SSD_CHUNK, SSD_HEADS // SSD_GROUPS
    xq = x.reshape(nb, nc, SSD_CHUNK, SSD_GROUPS, r, SSD_HEAD_DIM)
    dtq = dt.reshape(nb, nc, SSD_CHUNK, SSD_GROUPS, r)
    bq = bm.reshape(nb, nc, SSD_CHUNK, SSD_GROUPS, SSD_STATE)
    cq = cm.reshape(nb, nc, SSD_CHUNK, SSD_GROUPS, SSD_STATE)
    seg = a_cum[:, :, :, None] - a_cum[:, :, None, :]
    lower = jnp.tril(jnp.ones((SSD_CHUNK, SSD_CHUNK), bool))[:, :, None, None]
    decay = jnp.exp(jnp.where(lower, seg, -jnp.inf))
    cb = jnp.einsum('bcign,bcjgn->bcijg', cq, bq)
    w = cb[..., None] * decay * dtq[:, :, None]
    y_diag = jnp.einsum('bcijgr,bcjgrp->bcigrp', w, xq)
    y_off = jnp.einsum('bcign,bcgrpn->bcigrp', cq, h_start) * jnp.exp(a_cum)[..., None]
    return (y_diag + y_off).reshape(x.shape)


def ssd_branch(z, xbc, dt_raw, zc, xbcc, dtc_raw, conv_w, conv_b, a_log, dt_bias, d_skip, norm_g, ctx_out):
    a = -jnp.exp(a_log.astype(jnp.float32))

    def prep(xbc_s, dt_s):
        u = jax.nn.silu(dwconv(xbc_s, conv_w, conv_b))
        xs, bs, cs = jnp.split(u, [SSD_D_INNER, SSD_D_INNER + SSD_GROUPS * SSD_STATE], axis=-1)
        nb, n = u.shape[:2]
        dt = jax.nn.softplus(dt_s.astype(jnp.float32).reshape(nb, n, 2, SSD_HEADS)
                             + dt_bias.astype(jnp.float32))
        return (xs.reshape(nb, n, SSD_HEADS, SSD_HEAD_DIM),
                bs.reshape(nb, n, SSD_GROUPS, SSD_STATE),
                cs.reshape(nb, n, SSD_GROUPS, SSD_STATE), dt)

    xl, bl, cl, dtl = prep(xbc, dt_raw)
    xc, bc, cc, dtc = prep(xbcc, dtc_raw)
    h0 = jnp.zeros((xc.shape[0], SSD_GROUPS, SSD_HEADS // SSD_GROUPS, SSD_HEAD_DIM, SSD_STATE),
                   jnp.float32)
    y_lat = d_skip[:, None] * xl
    y_ctx = d_skip[:, None] * xc if ctx_out else None
    for d in range(2):
        rev = (lambda t: jnp.flip(t, axis=1)) if d == 1 else (lambda t: t)
        xcd, bcd, dtcd = rev(xc), rev(bc), rev(dtc[:, :, d])
        a_cum_c, h_start_c, h_ctx = ssd_states(xcd, dtcd, a[d], bcd, h0)
        if ctx_out:
            y_ctx = y_ctx + rev(ssd_outputs(xcd, dtcd, rev(cc), bcd, a_cum_c, h_start_c))
        xld, bld, cld, dtld = rev(xl), rev(bl), rev(cl), rev(dtl[:, :, d])
        a_cum_l, h_start_l, _ = ssd_states(xld, dtld, a[d], bld, h_ctx)
        y_lat = y_lat + rev(ssd_outputs(xld, dtld, cld, bld, a_cum_l, h_start_l))

    def finish(y, zz):
        return rms_norm(y.reshape(zz.shape) * jax.nn.silu(zz), norm_g)

    return finish(y_lat, z), (finish(y_ctx, zc) if ctx_out else None)


def na_branch(q, k, v, qc, kc, vc, rpb, ctx_out):
    nb, n = q.shape[:2]
    rows = n // GRID_W
    kh = min(NA_WIN_ROWS, rows)
    nblk = GRID_W // NA_COL_BLOCK
    slab = 2 * NA_WIN_COLS
    scale = HEAD_DIM ** -0.5
    r = jnp.arange(rows)
    row_idx = jnp.clip(r - kh // 2, 0, rows - kh)[:, None] + jnp.arange(kh)
    col_start = jnp.clip(jnp.arange(nblk) * NA_COL_BLOCK - NA_WIN_COLS // 2, 0, GRID_W - slab)
    col_idx = col_start[:, None] + jnp.arange(slab)
    qcol = jnp.arange(nblk)[:, None] * NA_COL_BLOCK + jnp.arange(NA_COL_BLOCK)
    win_start = jnp.clip(qcol - NA_WIN_COLS // 2, 0, GRID_W - NA_WIN_COLS)
    rel = col_idx[:, None, :] - win_start[:, :, None]
    in_win = (rel >= 0) & (rel < NA_WIN_COLS)
    dcol = jnp.clip(col_idx[:, None, :] - qcol[:, :, None] + NA_WIN_COLS - 1, 0, 2 * NA_WIN_COLS - 2)
    drow = row_idx - r[:, None] + NA_WIN_ROWS - 1
    bias = rpb.astype(jnp.float32)[:, drow[:, None, None, :, None], dcol[None, :, :, None, :]]

    k_g = k.reshape(nb, rows, GRID_W, NA_HEADS, HEAD_DIM)
    v_g = v.reshape(nb, rows, GRID_W, NA_HEADS, HEAD_DIM)
    gi_r, gi_c = row_idx[:, None, :, None], col_idx[None, :, None, :]
    k_nb = k_g[:, gi_r, gi_c]
    v_nb = v_g[:, gi_r, gi_c]
    q_b = q.reshape(nb, rows, nblk, NA_COL_BLOCK, NA_HEADS, HEAD_DIM)
    s_lat = jnp.einsum('brjqhd,brjkwhd->bhrjqkw', q_b, k_nb).astype(jnp.float32) * scale + bias
    s_lat = jnp.where(in_win[:, :, None, :], s_lat, -jnp.inf)
    s_lat = s_lat.reshape(nb, NA_HEADS, rows, nblk, NA_COL_BLOCK, kh * slab)
    s_ctx = jnp.einsum('brjqhd,bkhd->bhrjqk', q_b, kc).astype(jnp.float32) * scale
    p = jax.nn.softmax(jnp.concatenate([s_lat, s_ctx], -1), axis=-1)
    p_lat = p[..., :kh * slab].reshape(nb, NA_HEADS, rows, nblk, NA_COL_BLOCK, kh, slab).astype(v.dtype)
    p_ctx = p[..., kh * slab:].astype(v.dtype)
    o = (jnp.einsum('bhrjqkw,brjkwhd->brjqhd', p_lat, v_nb)
         + jnp.einsum('bhrjqk,bkhd->brjqhd', p_ctx, vc))
    o = o.reshape(nb, n, NA_WIDTH)
    return o, (ctx_attention(qc, kc, vc) if ctx_out else None)


def swa_branch(q, k, v, qc, kc, vc, sink, ctx_out):
    nb, n = q.shape[:2]
    nblk = n // SWA_BLOCK
    g = SWA_HEADS // SWA_KV_HEADS
    scale = HEAD_DIM ** -0.5
    q_b = q.reshape(nb, nblk, SWA_BLOCK, SWA_KV_HEADS, g, HEAD_DIM)

    def band(t):
        tp = jnp.pad(t, ((0, 0), (SWA_BLOCK, SWA_BLOCK), (0, 0), (0, 0)))
        tb = tp.reshape(nb, nblk + 2, SWA_BLOCK, SWA_KV_HEADS, HEAD_DIM)
        return jnp.concatenate([tb[:, :-2], tb[:, 1:-1], tb[:, 2:]], axis=2)

    k_band, v_band = band(k), band(v)
    nk = 3 * SWA_BLOCK
    key_pos = jnp.arange(nblk)[:, None] * SWA_BLOCK - SWA_BLOCK + jnp.arange(nk)
    q_pos = jnp.arange(nblk)[:, None] * SWA_BLOCK + jnp.arange(SWA_BLOCK)
    kp = key_pos[:, None, :]
    valid = (jnp.abs(kp - q_pos[:, :, None]) <= SWA_WINDOW) & (kp >= 0) & (kp < n)
    s_lat = jnp.einsum('bnqhgd,bnkhd->bhgnqk', q_b, k_band).astype(jnp.float32) * scale
    s_lat = jnp.where(valid, s_lat, -jnp.inf)
    s_ctx = jnp.einsum('bnqhgd,bkhd->bhgnqk', q_b, kc).astype(jnp.float32) * scale
    s_sink = jnp.broadcast_to(sink.astype(jnp.float32).reshape(SWA_KV_HEADS, g)[None, :, :, None, None, None],
                              s_lat.shape[:-1] + (1,))
    p = jax.nn.softmax(jnp.concatenate([s_lat, s_ctx, s_sink], -1), axis=-1)
    n_ctx = kc.shape[1]
    p_lat = p[..., :nk].astype(v.dtype)
    p_ctx = p[..., nk:nk + n_ctx].astype(v.dtype)
    o = (jnp.einsum('bhgnqk,bnkhd->bnqhgd', p_lat, v_band)
         + jnp.einsum('bhgnqk,bkhd->bnqhgd', p_ctx, vc))
    o = o.reshape(nb, n, SWA_WIDTH)
    return o, (ctx_attention(qc, kc, vc, sink) if ctx_out else None)


def fourier_mix(h):
    hf = h.astype(jnp.float32).reshape(*h.shape[:2], FN_GROUPS, FN_GROUP_DIM)
    y = jnp.fft.fft2(hf, axes=(1, 3), norm="ortho").real
    return y.reshape(h.shape).astype(h.dtype)


def conv_ffn(h, w_up, conv_w, conv_b, w_down):
    gate, up = jnp.split(h @ w_up, 2, axis=-1)
    return (jax.nn.silu(dwconv(gate, conv_w, conv_b)) * up) @ w_down


def mixer_block(h, hc, cos, sin, ctx_out, w_in, b_gate, ssd_conv_w, ssd_conv_b, ssd_a_log,
                ssd_dt_bias, ssd_d, ssd_norm_g, na_rpb, swa_sink, w_branch, w_out):
    split_at = [int(s) for s in np.cumsum(IN_SIZES)[:-1]]
    z, xbc, dtr, na_qkv, swa_q, swa_kv, fn_in, gates = jnp.split(h @ w_in, split_at, axis=-1)
    zc, xbcc, dtrc, na_qkvc, swa_qc, swa_kvc, fn_inc, gates_c = jnp.split(hc @ w_in, split_at, axis=-1)

    y_ssd, y_ssd_c = ssd_branch(z, xbc, dtr, zc, xbcc, dtrc, ssd_conv_w, ssd_conv_b, ssd_a_log,
                                ssd_dt_bias, ssd_d, ssd_norm_g, ctx_out)

    qn, kn, vn = [_heads(t) for t in jnp.split(na_qkv, 3, axis=-1)]
    qnc, knc, vnc = [_heads(t) for t in jnp.split(na_qkvc, 3, axis=-1)]
    y_na, y_na_c = na_branch(qn, kn, vn, qnc, knc, vnc, na_rpb, ctx_out)

    qs = apply_rope(_heads(swa_q), cos, sin)
    ks, vs = [_heads(t) for t in jnp.split(swa_kv, 2, axis=-1)]
    ks = apply_rope(ks, cos, sin)
    qsc = _heads(swa_qc)
    ksc, vsc = [_heads(t) for t in jnp.split(swa_kvc, 2, axis=-1)]
    y_swa, y_swa_c = swa_branch(qs, ks, vs, qsc, ksc, vsc, swa_sink, ctx_out)

    def merge(ys, g_raw):
        g = jax.nn.sigmoid(g_raw + b_gate).reshape(*g_raw.shape[:-1], N_BRANCH, D_MODEL)
        acc = g[..., 0, :] * (ys[0] @ w_branch[0])
        for i in range(1, N_BRANCH):
            acc = acc + g[..., i, :] * (ys[i] @ w_branch[i])
        return acc @ w_out

    out = merge((y_ssd, y_na, y_swa, fourier_mix(fn_in)), gates)
    out_c = merge((y_ssd_c, y_na_c, y_swa_c, fourier_mix(fn_inc)), gates_c) if ctx_out else None
    return out, out_c


def setup_inputs(seed: int = 0) -> dict:
    key = jax.random.key(seed)
    ks = jax.random.split(key, 32)
    f32 = jnp.float32
    D = D_MODEL

    def nrm(k, shape, std):
        return jax.random.normal(k, shape, f32) * std

    dt0 = jnp.exp(jax.random.uniform(ks[11], (DEPTH, 2, SSD_HEADS), f32, math.log(1e-3), math.log(1e-1)))
    return {
        "x": nrm(ks[0], (BATCH, SEQ, D), 1.0),
        "c": nrm(ks[1], (BATCH, D), 1.0),
        "ctx": nrm(ks[2], (BATCH, CTX_LEN, D), 1.0),
        "c_ctx": nrm(ks[3], (D,), 1.0),
        "w_ada": nrm(ks[4], (DEPTH, D, 6 * D), 0.01),
        "b_ada": nrm(ks[5], (DEPTH, 6 * D), 0.02),
        "w_in": nrm(ks[6], (DEPTH, D, IN_COLS), D ** -0.5),
        "b_gate": nrm(ks[7], (DEPTH, N_BRANCH * D), 0.02),
        "ssd_conv_w": nrm(ks[8], (DEPTH, SSD_CONV, SSD_XBC), SSD_CONV ** -0.5),
        "ssd_conv_b": nrm(ks[9], (DEPTH, SSD_XBC), 0.02),
        "ssd_a_log": jnp.log(jax.random.uniform(ks[10], (DEPTH, 2, SSD_HEADS), f32, 1.0, 16.0)),
        "ssd_dt_bias": dt0 + jnp.log(-jnp.expm1(-dt0)),
        "ssd_d": 1.0 + nrm(ks[12], (DEPTH, SSD_HEADS), 0.1),
        "ssd_norm_g": 1.0 + nrm(ks[13], (DEPTH, SSD_D_INNER), 0.05),
        "na_rpb": nrm(ks[14], (DEPTH, NA_HEADS, 2 * NA_WIN_ROWS - 1, 2 * NA_WIN_COLS - 1), 0.02),
        "swa_sink": nrm(ks[15], (DEPTH, SWA_HEADS), 0.5),
        "w_branch": nrm(ks[16], (DEPTH, N_BRANCH, BRANCH_WIDTH, D), BRANCH_WIDTH ** -0.5),
        "w_out": nrm(ks[17], (DEPTH, D, D), D ** -0.5 * BETA),
        "ln1_g": 1.0 + nrm(ks[18], (DEPTH, D), 0.05),
        "ln1_b": nrm(ks[19], (DEPTH, D), 0.02),
        "ln2_g": 1.0 + nrm(ks[20], (DEPTH, D), 0.05),
        "ln2_b": nrm(ks[21], (DEPTH, D), 0.02),
        "ffn_w_up": nrm(ks[22], (DEPTH, D, 2 * D_FF), D ** -0.5),
        "ffn_conv_w": nrm(ks[23], (DEPTH, FFN_CONV, D_FF), FFN_CONV ** -0.5),
        "ffn_conv_b": nrm(ks[24], (DEPTH, D_FF), 0.02),
        "ffn_w_down": nrm(ks[25], (DEPTH, D_FF, D), D_FF ** -0.5 * BETA),
    }


def reference(x, c, ctx, c_ctx, w_ada, b_ada, w_in, b_gate, ssd_conv_w, ssd_conv_b, ssd_a_log,
              ssd_dt_bias, ssd_d, ssd_norm_g, na_rpb, swa_sink, w_branch, w_out, ln1_g, ln1_b,
              ln2_g, ln2_b, ffn_w_up, ffn_conv_w, ffn_conv_b, ffn_w_down):
    n = x.shape[1]
    cos, sin = grid_rope(n)
    silu_c = jax.nn.silu(c)
    silu_cc = jax.nn.silu(c_ctx)
    xc = ctx
    for l in range(DEPTH):
        ctx_out = l < DEPTH - 1
        sh1, sc1, g1, sh2, sc2, g2 = jnp.split((silu_c @ w_ada[l] + b_ada[l])[:, None, :], 6, axis=-1)
        csh1, csc1, cg1, csh2, csc2, cg2 = jnp.split((silu_cc @ w_ada[l] + b_ada[l])[None, None, :], 6, axis=-1)
        h = layer_norm(x) * (1 + sc1) + sh1
        hc = layer_norm(xc) * (1 + csc1) + csh1
        mix, mix_c = mixer_block(h, hc, cos, sin, ctx_out, w_in[l], b_gate[l], ssd_conv_w[l],
                                 ssd_conv_b[l], ssd_a_log[l], ssd_dt_bias[l], ssd_d[l], ssd_norm_g[l],
                                 na_rpb[l], swa_sink[l], w_branch[l], w_out[l])
        x = layer_norm(ALPHA * x + g1 * mix, ln1_g[l], ln1_b[l])
        h = layer_norm(x) * (1 + sc2) + sh2
        x = layer_norm(ALPHA * x + g2 * conv_ffn(h, ffn_w_up[l], ffn_conv_w[l], ffn_conv_b[l], ffn_w_down[l]),
                       ln2_g[l], ln2_b[l])
        if ctx_out:
            xc = layer_norm(ALPHA * xc + cg1 * mix_c, ln1_g[l], ln1_b[l])
            hc = layer_norm(xc) * (1 + csc2) + csh2
            xc = layer_norm(ALPHA * xc + cg2 * conv_ffn(hc, ffn_w_up[l], ffn_conv_w[l], ffn_conv_b[l],
                                                        ffn_w_down[l]), ln2_g[l], ln2_b[l])
    return x
```

```python
from contextlib import ExitStack
import math
import numpy as np
import ml_dtypes
import concourse.bass as bass
import concourse.mybir as mybir
from concourse.bass_utils import run_bass_kernel_spmd

F32 = mybir.dt.float32
BF16 = mybir.dt.bfloat16
AF = mybir.ActivationFunctionType
ALU = mybir.AluOpType
AX = mybir.AxisListType

D = 2048
NCTX = 256
NLAT = 4096
T = NCTX + NLAT
DEPTH = 2
KC = D // 128
GRID_W = 64
D_FF = 5632
FC = D_FF // 128
ALPHA = (2.0 * DEPTH) ** 0.25
LN_EPS = 1e-6
TILES = [(0, 256)] + [(256 + 512 * i, 512) for i in range(8)]
NCORES = 4
STQ = "act"

_Z0, _XBC0, _DT0, _NQ0, _NK0, _NV0, _SQ0, _SK0, _SV0, _FN0, _G0 = 0, 512, 1536, 1552, 2064, 2576, 3088, 3600, 3728, 3856, 4368


def _partner(d):
    ax, half, f = d // 32, (d % 32) // 16, d % 16
    return ax * 32 + (1 - half) * 16 + f


def _w_in_perm():
    cols = []
    cols += list(range(_XBC0, _XBC0 + 1024))
    cols += list(range(_NQ0, _NQ0 + 512))
    cols += list(range(_NK0, _NK0 + 512))
    for j in range(4):
        blk = list(range(_SQ0 + 128 * j, _SQ0 + 128 * (j + 1)))
        cols += blk
        cols += [_SQ0 + 128 * j + 64 * (i // 64) + _partner(i % 64) for i in range(128)]
    cols += list(range(_FN0, _FN0 + 512))
    cols += list(range(_G0, _G0 + 8192))
    cols += list(range(_SK0, _SK0 + 128))
    cols += [_SK0 + 64 * (i // 64) + _partner(i % 64) for i in range(128)]
    n_fm = len(cols)
    cols += list(range(_Z0, _Z0 + 512))
    cols += list(range(_NV0, _NV0 + 512))
    cols += list(range(_SV0, _SV0 + 128))
    cols += list(range(_DT0, _DT0 + 16))
    return np.array(cols), n_fm


W_PERM, N_FM = _w_in_perm()
N_IN = len(W_PERM)
assert N_FM == 12032 and N_IN == 13200
FM_GROUPS = ([("xbc", i) for i in range(2)] + [("nq", 0), ("nk", 0)] + [("rope", i) for i in range(2)]
             + [("fn", 0)] + [("gate", i) for i in range(16)] + [("ropek", 0)])


def _na_geometry():
    rows = NLAT // GRID_W
    cfg_map = {}
    blocks = []
    cfgs = []
    qrl = np.repeat(np.arange(2), 64)
    qc = np.tile(np.arange(64), 2)
    for qt in range(rows // 2):
        r = 2 * qt + qrl
        rstart = np.clip(r - 4, 0, rows - 8)
        wstart = np.clip(qc - 8, 0, GRID_W - 16)
        kt_lo = int(rstart.min()) // 2
        kt_hi = int(rstart.max() + 7) // 2
        entry = []
        for kt in range(kt_lo, kt_hi + 1):
            kr = 2 * kt + qrl
            kc = qc
            valid = ((kr[:, None] >= rstart[None, :]) & (kr[:, None] < rstart[None, :] + 8)
                     & (kc[:, None] >= wstart[None, :]) & (kc[:, None] < wstart[None, :] + 16))
            entry.append((kt, 2 * (kt - qt), valid))
        key = tuple((d, v.tobytes()) for _, d, v in entry)
        if key not in cfg_map:
            cfg_map[key] = len(blocks)
            for _, d, v in entry:
                blocks.append(((d + 6) // 2, v))
        cfgs.append((cfg_map[key], [kt for kt, _, _ in entry]))
    return cfgs, blocks


NA_CFG, _NA_BLOCKS = _na_geometry()
NA_NB = len(_NA_BLOCKS)
NA_BLOCK_DI = [b[0] for b in _NA_BLOCKS]


def _na_bias_index():
    qrl = np.repeat(np.arange(2), 64)
    qc = np.tile(np.arange(64), 2)
    dr = np.zeros((7, 128, 128), np.int64)
    dc = np.zeros((7, 128, 128), np.int64)
    for di in range(7):
        delta = 2 * di - 6
        dr[di] = np.clip(delta + qrl[:, None] - qrl[None, :] + 7, 0, 14)
        dc[di] = np.clip(qc[:, None] - qc[None, :] + 15, 0, 30)
    return dr, dc


class Sched:
    COMPUTE = ("pe", "act", "dve", "pool")
    NDMA = 12

    def __init__(self, nc, stack):
        self.nc = nc
        self.stack = stack
        self.eng = {"pe": nc.tensor, "act": nc.scalar, "dve": nc.vector, "pool": nc.gpsimd, "sp": nc.sync}
        self.ops = []
        self.buf = {}
        self.dbuf = {}
        self.dram_names = set()
        self.bg_ops = set()
        self.psum_names = set()
        self.nsb = 0
        self.emitted = 0
        self.sem = {e: stack.enter_context(nc.semaphore(f"s_{e}")) for e in self.COMPUTE}
        self.dsem = {q: [stack.enter_context(nc.semaphore(f"d_{q}_{j}")) for j in range(self.NDMA)]
                     for q in ("sp", "pool", "act")}
        self.tok = []
        self.extra = []
        self.cnt = {e: 0 for e in self.COMPUTE}
        self.dcnt = {q: 0 for q in self.dsem}
        self.duse = {q: [0] * self.NDMA for q in self.dsem}
        self.dprev = {q: [None] * self.NDMA for q in self.dsem}
        self.seen = {e: {} for e in self.eng}

    def dram(self, name, shape, dtype, kind="Internal"):
        self.dram_names.add(name)
        return self.nc.dram_tensor(name, list(shape), dtype, kind=kind).ap()

    def sb(self, shape, dtype, name=None, stack=None):
        self.nsb += 1
        name = (name or "sb") + f"_{self.nsb}"
        return (stack or self.stack).enter_context(self.nc.sbuf_tensor(name, list(shape), dtype))

    def ps(self, shape, dtype, name=None, stack=None):
        self.nsb += 1
        name = (name or "ps") + f"_{self.nsb}"
        t = (stack or self.stack).enter_context(self.nc.psum_tensor(name, list(shape), dtype))
        self.psum_names.add(t[:].tensor.name)
        return t

    def _key(self, ap):
        if isinstance(ap, (str, tuple)):
            return ap
        return ap.tensor.name

    def _add(self, eng, fn, reads, writes, is_dma):
        deps = set()
        rk = [self._key(a) for a in reads if a is not None and not isinstance(a, (int, float))]
        wk = [self._key(a) for a in writes]
        for k in rk:
            if k in self.dram_names:
                b = self.dbuf.get(k)
                if b:
                    deps.update(b[0])
            else:
                b = self.buf.get(k)
                if b is not None and b[0] is not None:
                    deps.add(b[0])
                if b is not None and k in self.psum_names:
                    deps.update(v for e2, v in b[1].items() if e2 != eng)
        for k in wk:
            if k in self.dram_names:
                b = self.dbuf.get(k)
                if b:
                    deps.update(b[1])
            else:
                b = self.buf.get(k)
                if b is not None:
                    if b[0] is not None:
                        deps.add(b[0])
                    deps.update(b[1].values())
                    deps.update(b[2])
        i = len(self.ops)
        self.ops.append([eng, fn, deps, is_dma])
        for k in rk:
            if k in self.dram_names:
                self.dbuf.setdefault(k, [[], []])[1].append(i)
            else:
                b = self.buf.setdefault(k, [None, {}, []])
                if is_dma:
                    b[2].append(i)
                else:
                    b[1][eng] = i
        for k in wk:
            if k in self.dram_names:
                self.dbuf.setdefault(k, [[], []])[0].append(i)
            else:
                self.buf[k] = [i, {}, []]
        return i

    def op(self, eng, fn, reads=(), writes=()):
        return self._add(eng, fn, reads, writes, False)

    def dma(self, out, in_, queue="sp", reads=None, writes=None, bg=False):
        i = self._add(queue, lambda e: e.dma_start(out=out, in_=in_),
                      [in_] if reads is None else reads, [out] if writes is None else writes, True)
        if bg:
            self.bg_ops.add(i)
        return i

    def mm(self, out, lhsT, rhs, start=True, stop=True):
        return self.op("pe", lambda e: e.matmul(out, lhsT, rhs, start=start, stop=stop), [lhsT, rhs], [out])

    def tr(self, out, in_, ident):
        return self.op("pe", lambda e: e.transpose(out, in_, ident), [in_, ident], [out])

    def act(self, out, in_, func, bias=None, scale=None, accum_out=None):
        kw = {}
        if bias is not None:
            kw["bias"] = bias
        if scale is not None:
            kw["scale"] = scale
        if accum_out is not None:
            kw["accum_out"] = accum_out
        rd = [in_] + [a for a in (bias, scale) if a is not None and not isinstance(a, (int, float))]
        wr = [out] + ([accum_out] if accum_out is not None else [])
        return self.op("act", lambda e: e.activation(out, in_, func, **kw), rd, wr)

    def tt(self, out, in0, in1, op, eng="dve"):
        return self.op(eng, lambda e: e.tensor_tensor(out, in0, in1, op), [in0, in1], [out])

    def ts(self, out, in0, s1, s2=None, op0=ALU.mult, op1=None, eng="dve"):
        kw = {}
        if op1 is not None:
            kw["op1"] = op1
        rd = [in0] + [a for a in (s1, s2) if a is not None and not isinstance(a, (int, float))]
        return self.op(eng, lambda e: e.tensor_scalar(out, in0, s1, s2, op0, **kw), rd, [out])

    def stt(self, out, in0, scalar, in1, op0, op1, eng="dve"):
        rd = [in0, in1] + ([scalar] if not isinstance(scalar, (int, float)) else [])
        return self.op(eng, lambda e: e.scalar_tensor_tensor(out, in0, scalar, in1, op0, op1), rd, [out])

    def cp(self, out, in_, eng="dve"):
        if eng == "act":
            return self.op("act", lambda e: e.copy(out, in_), [in_], [out])
        return self.op(eng, lambda e: e.tensor_copy(out, in_), [in_], [out])

    def memset(self, out, val, eng="dve"):
        return self.op(eng, lambda e: e.memset(out, val), [], [out])

    def recip(self, out, in_):
        return self.op("dve", lambda e: e.reciprocal(out, in_), [in_], [out])

    def sqrt(self, out, in_):
        return self.op("act", lambda e: e.sqrt(out, in_), [in_], [out])

    def flush(self, final=False):
        nc = self.nc
        ops = self.ops
        lo, n = self.emitted, len(self.ops)
        if n == lo:
            return 0
        needs = [False] * (n - lo)
        for i in range(lo, n):
            eng, fn, deps, is_dma = ops[i]
            for d in deps:
                de, _, _, ddma = ops[d]
                if ddma or d < lo:
                    continue
                if de == eng and eng == "pe" and not is_dma:
                    continue
                needs[d - lo] = True
        streams = {e: [] for e in self.eng}
        for i in range(lo, n):
            streams[ops[i][0]].append(i)
        for e in self.COMPUTE:
            if streams[e]:
                needs[streams[e][-1] - lo] = True
        self.tok.extend([None] * (n - lo))
        self.extra.extend([None] * (n - lo))
        tok, extra = self.tok, self.extra
        for i in range(lo, n):
            eng, fn, deps, is_dma = ops[i]
            if is_dma:
                j = self.dcnt[eng] % self.NDMA
                self.dcnt[eng] += 1
                self.duse[eng][j] += 1
                tok[i] = (self.dsem[eng][j], 16 * self.duse[eng][j])
                extra[i] = self.dprev[eng][j]
                self.dprev[eng][j] = i
            elif needs[i - lo]:
                self.cnt[eng] += 1
                tok[i] = (self.sem[eng], self.cnt[eng])
        bar = [(self.sem[e], self.cnt[e]) for e in self.COMPUTE if self.cnt[e] > 0]
        for q in self.dsem:
            if q == "pool" and not final:
                continue
            for j in range(self.NDMA):
                if self.duse[q][j] > 0:
                    bar.append((self.dsem[q][j], 16 * self.duse[q][j]))

        def run(ename):
            def body(e):
                seen = self.seen[ename]
                for i in streams[ename]:
                    eng, fn, deps, is_dma = ops[i]
                    want = {}
                    dl = list(deps)
                    if extra[i] is not None:
                        dl.append(extra[i])
                    for d in dl:
                        if tok[d] is None:
                            continue
                        s, v = tok[d]
                        if seen.get(s.num, 0) >= v:
                            continue
                        if want.get(s.num, (None, 0))[1] < v:
                            want[s.num] = (s, v)
                    for s, v in want.values():
                        e.wait_ge(s, v)
                        seen[s.num] = v
                    ins = fn(e)
                    if tok[i] is not None:
                        ins.then_inc(tok[i][0], 16 if is_dma else 1)
                for s, v in bar:
                    if ename in self.COMPUTE and s.num == self.sem[ename].num:
                        continue
                    if seen.get(s.num, 0) < v:
                        e.wait_ge(s, v)
                        seen[s.num] = v
            return body

        with nc.Block() as block:
            block.sync(run("sp"))
            block.tensor(run("pe"))
            block.scalar(run("act"))
            block.vector(run("dve"))
            block.gpsimd(run("pool"))
        self.emitted = n
        self.buf.clear()
        keep = {}
        for k, b in self.dbuf.items():
            w = [i for i in b[0] if i in self.bg_ops]
            if w:
                keep[k] = [w, []]
        self.dbuf = keep
        for i in range(lo, n):
            ops[i][1] = None
            ops[i][2] = None
        return n - lo


class Prog:
    def __init__(self, debug=(), layers=(0, 1), stop_after=None, feed=(), only=None):
        self.debug = set(debug)
        self.feed = set(feed)
        self.only = only
        self.layers = layers
        self.stop_after = stop_after
        self.nc = bass.Bass("TRN2", target_bir_lowering=False)
        self.stack = ExitStack()
        self.s = Sched(self.nc, self.stack)
        self.outs = []

    def din(self, name, shape, dtype=F32):
        return self.s.dram(name, shape, dtype, kind="ExternalInput")

    def scratch(self, name, shape, dtype):
        if name in self.feed:
            return self.s.dram(name, shape, dtype, kind="ExternalInput")
        if name in self.debug:
            self.outs.append(name)
            return self.s.dram(name, shape, dtype, kind="ExternalOutput")
        return self.s.dram(name, shape, dtype, kind="Internal")

    def declare(self):
        L = DEPTH
        self.xin = self.din("xin", [T, D])
        self.cT = self.din("cT", [128, KC, 2])
        self.w_ada = self.din("w_ada", [L, D, 6 * D])
        self.b_ada_col = self.din("b_ada_col", [128, L, 96])
        self.b_ada = self.din("b_ada", [L, 6 * D])
        self.w_in_r = self.din("w_in_r", [L, D, N_IN])
        self.b_gate_col = self.din("b_gate_col", [128, L, 64])
        self.ident_in = self.din("ident", [128, 128])
        self.rope_cos = self.din("rope_cos", [128, NLAT])
        self.rope_sin = self.din("rope_sin", [128, NLAT])
        self.WIN = [self.scratch(f"WIN{l}", [D, N_IN], BF16) for l in range(L)]
        self.XS = self.scratch("XS", [T, D], F32)
        self.MODR = self.scratch("MODR", [L, 2, 2, D], F32)
        self.XBCT = self.scratch("XBCT", [1024, T], F32)
        self.NQT = self.scratch("NQT", [512, T], BF16)
        self.NKT = self.scratch("NKT", [512, T], BF16)
        self.SQT = self.scratch("SQT", [512, T], BF16)
        self.SKT = self.scratch("SKT", [128, T], BF16)
        self.FNT = self.scratch("FNT", [512, T], BF16)
        self.GT = self.scratch("GT", [8192, T], BF16)
        self.ZS = self.scratch("ZS", [T, 512], F32)
        self.NV = self.scratch("NV", [T, 512], BF16)
        self.SV = self.scratch("SV", [T, 128], BF16)
        self.DTR = self.scratch("DTR", [T, 16], F32)
        self.YT = [self.scratch(f"YT{i}", [512, T], BF16) for i in range(4)]
        self.XTM = self.scratch("XTM", [T, 512], F32)
        self.ACCT = self.scratch("ACCT", [D, T], BF16)
        self.X1 = self.scratch("X1", [T, D], F32)
        self.H2T = self.scratch("H2T", [D, T], BF16)
        self.w_branch = self.din("w_branch", [L, 4 * 512, D])
        self.w_out = self.din("w_out", [L, D, D])
        self.ffn_w_up = self.din("ffn_w_up", [L, D, 2 * D_FF])
        self.ffn_w_down = self.din("ffn_w_down", [L, D_FF, D])
        self.WBR = [self.scratch(f"WBR{l}", [4 * 512, D], BF16) for l in range(L)]
        self.WOUT = [self.scratch(f"WOUT{l}", [D, D], BF16) for l in range(L)]
        self.WUP = [self.scratch(f"WUP{l}", [D, 2 * D_FF], BF16) for l in range(L)]
        self.WDN = [self.scratch(f"WDN{l}", [D_FF, D], BF16) for l in range(L)]
        self.ln1_g = self.din("ln1_g", [L, D])
        self.ln1_b = self.din("ln1_b", [L, D])
        self.ln2_g = self.din("ln2_g", [L, D])
        self.ln2_b = self.din("ln2_b", [L, D])
        self.ffn_convw_col = self.din("ffn_convw_col", [128, L, FC, 3])
        self.ffn_convb_col = self.din("ffn_convb_col", [128, L, FC])
        self.out = self.s.dram("out", [NLAT, D], F32, kind="ExternalOutput")
        self.XTMB = self.scratch("XTMB", [T, 512], BF16)
        self.BTM = self.scratch("BTM", [T, 256], BF16)
        self.BCT = self.scratch("BCT", [512, T], BF16)
        self.YF = self.scratch("YF", [T, 512], F32)
        self.ssd_convw_col = self.din("ssd_convw_col", [128, L, 8, 5])
        self.ssd_convb_col = self.din("ssd_convb_col", [128, L, 8])
        self.ssd_dt_bias = self.din("ssd_dt_bias", [L, 16])
        self.ssd_a_log = self.din("ssd_a_log", [L, 16])
        self.ssd_d = self.din("ssd_d", [L, 8])
        self.ssd_norm_g = self.din("ssd_norm_g", [L, 512])
        self.tri_f = self.din("tri_f", [128, 128])
        self.tri_b = self.din("tri_b", [128, 128])
        self.cs128 = self.din("cs128", [128, 256], BF16)
        self.dftc = self.din("dftc", [NLAT, NLAT], BF16)
        self.dfts = self.din("dfts", [NLAT, NLAT], BF16)
        self.c256 = self.din("c256", [NCTX, NCTX], BF16)
        self.s256 = self.din("s256", [NCTX, NCTX], BF16)
        self.band_masks = self.din("band_masks", [128, 2, 128], BF16)
        self.swa_sink = self.din("swa_sink", [L, 8])
        self.na_masks = self.din("na_masks", [128, NA_NB, 128], BF16)
        self.na_bias = self.din("na_bias", [L, 4, 128, 2, 7, 128])

    def phase0(self, light=False):
        s = self.s
        L = DEPTH
        self.ident = s.sb([128, 128], BF16, "ident")
        self.ones_f = s.sb([128, 128], F32, "ones_f")
        self.modc = s.sb([128, L, 64, 2], F32, "modc")
        self.bgate = s.sb([128, L, 64], F32, "bgate")
        with ExitStack() as ph:
            idf = s.sb([128, 128], F32, stack=ph)
            s.dma(idf[:], self.ident_in)
            s.cp(self.ident[:], idf[:])
            s.memset(self.ones_f[:], 1.0)
            s.dma(self.bgate[:], self.b_gate_col)
            if light:
                s.flush()
                return
            for l in self.layers:
                for r in range(0, D, 128):
                    s.dma(self.WIN[l][r:r + 128, :], self.w_in_r[l, r:r + 128, :], queue="pool", bg=True)
                for r in range(0, D, 128):
                    s.dma(self.WBR[l][r:r + 128, :], self.w_branch[l, r:r + 128, :], queue="pool", bg=True)
                    s.dma(self.WOUT[l][r:r + 128, :], self.w_out[l, r:r + 128, :], queue="pool", bg=True)
                    s.dma(self.WUP[l][r:r + 128, :], self.ffn_w_up[l, r:r + 128, :], queue="pool", bg=True)
                for r in range(0, D_FF, 128):
                    s.dma(self.WDN[l][r:r + 128, :], self.ffn_w_down[l, r:r + 128, :], queue="pool", bg=True)
            cT = s.sb([128, KC, 2], F32, stack=ph)
            sT = s.sb([128, KC, 2], F32, stack=ph)
            srep = [s.sb([128, KC, 128], F32, stack=ph) for _ in range(2)]
            bcol = s.sb([128, L, 96], F32, stack=ph)
            brow = s.sb([128, 512], F32, stack=ph)
            wt = [s.sb([128, KC, 512], F32, stack=ph) for _ in range(2)]
            orow = s.sb([128, 512], F32, stack=ph)
            pcol = s.ps([128, 128], F32, stack=ph)
            prow = [s.ps([128, 512], F32, stack=ph) for _ in range(2)]
            s.dma(cT[:], self.cT)
            s.dma(bcol[:], self.b_ada_col)
            s.act(sT[:], cT[:], AF.Silu)
            for m in range(2):
                s.tt(srep[m][:], self.ones_f[:].unsqueeze(1).to_broadcast([128, KC, 128]),
                     sT[:, :, m:m + 1].to_broadcast([128, KC, 128]), ALU.mult)
            ng = 0
            for l in self.layers:
                for g in range(24):
                    c0 = g * 512
                    w = wt[ng % 2]
                    ng += 1
                    s.dma(w[:], self.w_ada[l, :, c0:c0 + 512].rearrange("(k p) n -> p k n", p=128))
                    sec = c0 // D
                    if sec in (2, 5):
                        s.dma(brow[:], self.b_ada[l, c0:c0 + 512].partition_broadcast(128))
                        for m in range(2):
                            p = prow[m]
                            for k in range(KC):
                                s.mm(p[:], srep[m][:, k, :], w[:, k, :], start=(k == 0), stop=(k == KC - 1))
                            s.tt(orow[:], p[:], brow[:], ALU.add)
                            gi = 0 if sec == 2 else 1
                            cc = c0 - sec * D
                            s.dma(self.MODR[l, gi, m, cc:cc + 512].unsqueeze(0), orow[0:1, :])
                    else:
                        cb = {0: 0, 1: 16, 3: 32, 4: 48}[sec] + (c0 % D) // 128
                        for j in range(4):
                            for k in range(KC):
                                s.mm(pcol[:, (cb + j) * 2:(cb + j) * 2 + 2], w[:, k, j * 128:(j + 1) * 128], sT[:, k, :],
                                     start=(k == 0), stop=(k == KC - 1))
                pv = pcol[:].rearrange("p (b m) -> p b m", m=2)
                for (o, srcb) in ((0, 0), (16, 16), (32, 48), (48, 64)):
                    s.tt(self.modc[:, l, o:o + 16, :], pv[:, o:o + 16, :],
                         bcol[:, l, srcb:srcb + 16].unsqueeze(2).to_broadcast([128, 16, 2]), ALU.add)
                for o in (16, 48):
                    s.ts(self.modc[:, l, o:o + 16, :], self.modc[:, l, o:o + 16, :], 1.0, None, op0=ALU.add)
            s.flush()

    def ln_rows(self, xt, yb, junk, st):
        s = self.s
        s.memset(st[:], 0.0)
        s.act(junk[:], xt[:], AF.Identity, accum_out=st[:, 0:1])
        s.act(junk[:], xt[:], AF.Square, accum_out=st[:, 1:2])
        s.ts(st[:, 2:3], st[:, 0:1], 1.0 / D, None, op0=ALU.mult)
        s.ts(st[:, 3:4], st[:, 1:2], 1.0 / D, None, op0=ALU.mult)
        s.tt(st[:, 4:5], st[:, 2:3], st[:, 2:3], ALU.mult)
        s.tt(st[:, 5:6], st[:, 3:4], st[:, 4:5], ALU.subtract)
        s.ts(st[:, 6:7], st[:, 5:6], LN_EPS, None, op0=ALU.add)
        s.sqrt(st[:, 6:7], st[:, 6:7])
        s.recip(st[:, 6:7], st[:, 6:7])
        s.stt(st[:, 7:8], st[:, 2:3], -1.0, st[:, 6:7], ALU.mult, ALU.mult)
        s.act(yb[:], xt[:], AF.Identity, bias=st[:, 7:8], scale=st[:, 6:7])

    def mod_transpose(self, yb, hT, col0, tp, tmp, sc, sh):
        s = self.s
        for half in range(2):
            p = tp[half]
            for kk in range(8):
                k = half * 8 + kk
                s.tr(p[:, kk * 128:(kk + 1) * 128], yb[:, k * 128:(k + 1) * 128], self.ident[:])
            pv = p[:].rearrange("p (k t) -> p k t", t=128)
            s.tt(tmp[:], pv, sc[:, half * 8:half * 8 + 8].unsqueeze(2).to_broadcast([128, 8, 128]), ALU.mult)
            s.tt(hT[:, half * 8:half * 8 + 8, col0:col0 + 128], tmp[:],
                 sh[:, half * 8:half * 8 + 8].unsqueeze(2).to_broadcast([128, 8, 128]), ALU.add)

    def phase_inproj(self, l, src):
        s = self.s
        with ExitStack() as ph:
            xt = [s.sb([128, D], F32, stack=ph) for _ in range(2)]
            yb = [s.sb([128, D], BF16, stack=ph) for _ in range(2)]
            junk = s.sb([128, D], BF16, stack=ph)
            st = [s.sb([128, 8], F32, stack=ph) for _ in range(2)]
            tmp = s.sb([128, 8, 128], F32, stack=ph)
            hT = [s.sb([128, KC, 512], BF16, stack=ph) for _ in range(2)]
            NWT = 4
            wt = [s.sb([128, KC, 512], BF16, stack=ph) for _ in range(NWT)]
            sg_f = [s.sb([128, 4, 512], F32, stack=ph) for _ in range(2)]
            sg_b = [s.sb([128, 4, 512], BF16, stack=ph) for _ in range(2)]
            cos_t = s.sb([128, 512], F32, stack=ph)
            sin_t = s.sb([128, 512], F32, stack=ph)
            r1 = s.sb([128, 512], F32, stack=ph)
            r2 = s.sb([128, 512], F32, stack=ph)
            tp = [s.ps([128, 1024], BF16, stack=ph) for _ in range(2)]
            pm = [s.ps([128, 512], F32, stack=ph) for _ in range(4)]
            WIN = self.WIN[l]
            nw = 0
            nsg = 0
            npm = 0
            nln = [0]

            def ln_sub(tj, a):
                tt0, _ = TILES[tj]
                mm_ = 1 if tj == 0 else 0
                x_, y_, st_ = xt[nln[0] % 2], yb[nln[0] % 2], st[nln[0] % 2]
                nln[0] += 1
                s.dma(x_[:], src[tt0 + a * 128:tt0 + (a + 1) * 128, :])
                self.ln_rows(x_, y_, junk, st_)
                self.mod_transpose(y_, hT[tj % 2], a * 128, tp, tmp, self.modc[:, l, 16:32, mm_], self.modc[:, l, 0:16, mm_])

            for a in range(TILES[0][1] // 128):
                ln_sub(0, a)
            LN_AT = {3: 0, 9: 1, 15: 2, 21: 3}
            for ti, (t0, nt) in enumerate(TILES):
                m = 1 if ti == 0 else 0
                is_ctx = ti == 0
                h = hT[ti % 2]
                if not is_ctx:
                    l0 = t0 - NCTX
                    s.dma(cos_t[:, :nt], self.rope_cos[:, l0:l0 + nt])
                    s.dma(sin_t[:, :nt], self.rope_sin[:, l0:l0 + nt])
                for gi, (kind, gidx) in enumerate(FM_GROUPS):
                    c0 = gi * 512
                    nb = 4 if kind != "ropek" else 2
                    if gi in LN_AT and ti + 1 < len(TILES):
                        ln_sub(ti + 1, LN_AT[gi])
                    w = wt[nw % NWT]
                    nw += 1
                    s.dma(w[:, :, :nb * 128], WIN[:, c0:c0 + nb * 128].rearrange("(k p) n -> p k n", p=128))
                    if kind in ("xbc",):
                        sg = sg_f[nsg % 2]
                    else:
                        sg = sg_b[nsg % 2]
                    nsg += 1
                    ps_blocks = []
                    for j in range(nb):
                        p = pm[npm % 4]
                        npm += 1
                        for k in range(KC):
                            s.mm(p[:, :nt], w[:, k, j * 128:(j + 1) * 128], h[:, k, :nt], start=(k == 0), stop=(k == KC - 1))
                        if kind == "xbc":
                            s.cp(sg[:, j, :nt], p[:, :nt], eng="act")
                        elif kind in ("nq", "nk", "fn"):
                            s.cp(sg[:, j, :nt], p[:, :nt], eng="act" if j % 2 == 0 else "dve")
                        elif kind == "gate":
                            blk = gidx * 4 + j
                            s.act(sg[:, j, :nt], p[:, :nt], AF.Sigmoid, bias=self.bgate[:, l, blk:blk + 1])
                        elif kind in ("rope", "ropek"):
                            if j % 2 == 0:
                                if is_ctx:
                                    s.cp(sg[:, j // 2, :nt], p[:, :nt], eng="act")
                                else:
                                    s.tt(r1[:, :nt], p[:, :nt], cos_t[:, :nt], ALU.mult)
                            else:
                                if not is_ctx:
                                    s.tt(r2[:, :nt], p[:, :nt], sin_t[:, :nt], ALU.mult)
                                    s.tt(sg[:, j // 2, :nt], r1[:, :nt], r2[:, :nt], ALU.add)
                    if kind == "xbc":
                        dst = self.XBCT[gidx * 512:(gidx + 1) * 512, t0:t0 + nt]
                        nbo = 4
                    elif kind == "nq":
                        dst, nbo = self.NQT[:, t0:t0 + nt], 4
                    elif kind == "nk":
                        dst, nbo = self.NKT[:, t0:t0 + nt], 4
                    elif kind == "fn":
                        dst, nbo = self.FNT[:, t0:t0 + nt], 4
                    elif kind == "gate":
                        dst, nbo = self.GT[gidx * 512:(gidx + 1) * 512, t0:t0 + nt], 4
                    elif kind == "rope":
                        dst, nbo = self.SQT[gidx * 256:(gidx + 1) * 256, t0:t0 + nt], 2
                    else:
                        dst, nbo = self.SKT[:, t0:t0 + nt], 1
                    s.dma(dst.rearrange("(j p) t -> p j t", p=128), sg[:, :nbo, :nt], queue=STQ)
                for (cs, ncol, kind) in ((N_FM, 512, "z"), (N_FM + 512, 512, "nv"), (N_FM + 1024, 144, "svdt")):
                    w = wt[nw % NWT]
                    nw += 1
                    s.dma(w[:, :, :ncol], WIN[:, cs:cs + ncol].rearrange("(k p) n -> p k n", p=128))
                    sg = sg_f[nsg % 2] if kind != "nv" else sg_b[nsg % 2]
                    sgb = sg_b[nsg % 2]
                    nsg += 1
                    for a in range(nt // 128):
                        p = pm[npm % 4]
                        npm += 1
                        for k in range(KC):
                            s.mm(p[:, :ncol], h[:, k, a * 128:(a + 1) * 128], w[:, k, :ncol], start=(k == 0), stop=(k == KC - 1))
                        if kind == "z":
                            s.act(sg[:, a, :], p[:], AF.Silu)
                        elif kind == "nv":
                            s.cp(sg[:, a, :], p[:], eng="dve")
                        else:
                            s.cp(sgb[:, a, 0:128], p[:, 0:128], eng="dve")
                            s.cp(sg[:, a, 0:16], p[:, 128:144], eng="dve")
                    na = nt // 128
                    if kind == "z":
                        s.dma(self.ZS[t0:t0 + nt, :].rearrange("(a p) n -> p a n", p=128), sg[:, :na, :], queue=STQ)
                    elif kind == "nv":
                        s.dma(self.NV[t0:t0 + nt, :].rearrange("(a p) n -> p a n", p=128), sg[:, :na, :], queue=STQ)
                    else:
                        s.dma(self.SV[t0:t0 + nt, :].rearrange("(a p) n -> p a n", p=128), sgb[:, :na, 0:128], queue=STQ)
                        s.dma(self.DTR[t0:t0 + nt, :].rearrange("(a p) n -> p a n", p=128), sg[:, :na, 0:16], queue=STQ)
            s.flush()

    def phase_fnet(self, l, ctx_out):
        s = self.s
        with ExitStack() as ph:
            U = s.sb([128, 34, 4, 256], BF16, stack=ph)
            with ExitStack() as ph1:
                fT = s.sb([128, 4, T], BF16, stack=ph1)
                cs = s.sb([128, 256], BF16, stack=ph1)
                pu = [s.ps([128, 512], F32, stack=ph1) for _ in range(4)]
                s.dma(cs[:], self.cs128)
                for g in range(4):
                    s.dma(fT[:, g, :], self.FNT[g * 128:(g + 1) * 128, :])
                n = 0
                for a in range(34):
                    for gp in range(2):
                        p = pu[n % 4]
                        n += 1
                        for gg in range(2):
                            s.mm(p[:, gg * 256:(gg + 1) * 256], fT[:, gp * 2 + gg, a * 128:(a + 1) * 128], cs[:])
                        s.cp(U[:, a, gp * 2:gp * 2 + 2, :], p[:].rearrange("p (g c) -> p g c", g=2), eng="act" if n % 2 else "dve")
                s.flush()
            with ExitStack() as ph2:
                cbuf = [s.sb([128, 8, 512], BF16, stack=ph2) for _ in range(4)]
                sbuf = [s.sb([128, 8, 512], BF16, stack=ph2) for _ in range(4)]
                ystage = [s.sb([128, 4, 512], BF16, stack=ph2) for _ in range(2)]
                c256 = s.sb([128, 2, 256], BF16, stack=ph2)
                s256 = s.sb([128, 2, 256], BF16, stack=ph2)
                pacc = [s.ps([128, 512], F32, stack=ph2) for _ in range(4)]
                YT = self.YT[3]
                if ctx_out:
                    s.dma(c256[:], self.c256.rearrange("(a p) k -> p a k", p=128))
                    s.dma(s256[:], self.s256.rearrange("(a p) k -> p a k", p=128))
                    ys = ystage[1]
                    for g in range(4):
                        for a in range(2):
                            s.mm(pacc[g][:, :256], U[:, a, g, 0:128], c256[:, a, :], start=(a == 0), stop=False)
                            s.mm(pacc[g][:, :256], U[:, a, g, 128:256], s256[:, a, :], start=False, stop=(a == 1))
                        s.cp(ys[:, g, :256], pacc[g][:, :256], eng="act" if g % 2 else "dve")
                    s.dma(YT[:, 0:256].rearrange("(g p) t -> p g t", p=128), ys[:, :, :256], queue=STQ)
                nb = 0
                for kb in range(8):
                    ys = ystage[kb % 2]
                    for ncn in range(4):
                        cb_, sb_ = cbuf[nb % 4], sbuf[nb % 4]
                        nb += 1
                        s.dma(cb_[:], self.dftc[ncn * 1024:(ncn + 1) * 1024, kb * 512:(kb + 1) * 512].rearrange("(a p) k -> p a k", p=128))
                        s.dma(sb_[:], self.dfts[ncn * 1024:(ncn + 1) * 1024, kb * 512:(kb + 1) * 512].rearrange("(a p) k -> p a k", p=128))
                        for g in range(4):
                            for a in range(8):
                                nt = ncn * 8 + a
                                s.mm(pacc[g][:], U[:, 2 + nt, g, 0:128], cb_[:, a, :], start=(nt == 0), stop=False)
                                s.mm(pacc[g][:], U[:, 2 + nt, g, 128:256], sb_[:, a, :], start=False, stop=(nt == 31))
                    for g in range(4):
                        s.cp(ys[:, g, :], pacc[g][:], eng="act" if g % 2 else "dve")
                    s.dma(YT[:, NCTX + kb * 512:NCTX + (kb + 1) * 512].rearrange("(g p) t -> p g t", p=128), ys[:], queue=STQ)
                s.flush()

    def phase_swa(self, l, ctx_out):
        s = self.s
        with ExitStack() as ph:
            QT = s.sb([64, 8, T], BF16, stack=ph)
            KT = s.sb([64, 2, T], BF16, stack=ph)
            Vraw = s.sb([128, 34, 128], BF16, stack=ph)
            Vaug = s.sb([128, 34, 2, 65], BF16, stack=ph)
            masks = s.sb([128, 2, 128], BF16, stack=ph)
            sinkb = s.sb([128, 8], F32, stack=ph)
            E = [s.sb([128, 5, 4, 128], BF16, stack=ph) for _ in range(2)]
            yb = [s.sb([128, 4, 64], BF16, stack=ph) for _ in range(2)]
            den = [s.sb([128, 4], F32, stack=ph) for _ in range(2)]
            stage = [s.sb([128, 2, 512], BF16, stack=ph) for _ in range(2)]
            pS = [s.ps([128, 512], F32, stack=ph) for _ in range(3)]
            pO = [s.ps([128, 512], F32, stack=ph) for _ in range(2)]
            pT = [s.ps([128, 1024], BF16, stack=ph) for _ in range(2)]
            for h in range(8):
                s.dma(QT[:, h, :], self.SQT[h * 64:(h + 1) * 64, :])
            for hk in range(2):
                s.dma(KT[:, hk, :], self.SKT[hk * 64:(hk + 1) * 64, :])
            s.dma(Vraw[:], self.SV.rearrange("(a p) c -> p a c", p=128))
            s.memset(Vaug[:], 1.0)
            s.cp(Vaug[:, :, :, 0:64], Vraw[:].rearrange("p a (h d) -> p a h d", h=2))
            s.dma(masks[:], self.band_masks)
            s.dma(sinkb[:], self.swa_sink[l].partition_broadcast(128))
            s.act(sinkb[:], sinkb[:], AF.Exp)
            YT = self.YT[2]
            qtiles = ([0, 1] if ctx_out else []) + list(range(2, 34))
            n = 0
            nps = 0
            for hk in range(2):
                nst = 0
                for qi, qt in enumerate(qtiles):
                    if qt < 2:
                        keys = [(0, None), (1, None)]
                    else:
                        keys = [(0, None), (1, None)]
                        if qt - 1 >= 2:
                            keys.append((qt - 1, 0))
                        keys.append((qt, None))
                        if qt + 1 <= 33:
                            keys.append((qt + 1, 1))
                    e = E[n % 2]
                    po = pO[n % 2]
                    y_ = yb[n % 2]
                    dn = den[n % 2]
                    n += 1
                    for ki, (kt, mk) in enumerate(keys):
                        p = pS[nps % 3]
                        nps += 1
                        for g in range(4):
                            s.mm(p[:, g * 128:(g + 1) * 128], KT[:, hk, kt * 128:(kt + 1) * 128],
                                 QT[:, 4 * hk + g, qt * 128:(qt + 1) * 128])
                        s.act(e[:, ki, :, :], p[:].rearrange("p (g q) -> p g q", g=4), AF.Exp, scale=0.125)
                        if mk is not None:
                            s.tt(e[:, ki, :, :], e[:, ki, :, :], masks[:, mk, :].unsqueeze(1).to_broadcast([128, 4, 128]), ALU.mult)
                    for g in range(4):
                        for ki, (kt, mk) in enumerate(keys):
                            s.mm(po[:, g * 65:(g + 1) * 65], e[:, ki, g, :], Vaug[:, kt, hk, :], start=(ki == 0), stop=(ki == len(keys) - 1))
                    pov = po[:, 0:260].rearrange("p (g c) -> p g c", c=65)
                    s.tt(dn[:], pov[:, :, 64], sinkb[:, 4 * hk:4 * hk + 4], ALU.add)
                    s.recip(dn[:], dn[:])
                    s.tt(y_[:], pov[:, :, 0:64], dn[:].unsqueeze(2).to_broadcast([128, 4, 64]), ALU.mult)
                    pt = pT[n % 2]
                    yv = y_[:].rearrange("p g d -> p (g d)")
                    for j in range(2):
                        s.tr(pt[:, j * 128:(j + 1) * 128], yv[:, j * 128:(j + 1) * 128], self.ident[:])
                    first_lat = qtiles.index(2)
                    if qt < 2:
                        slot, t0g, width, last = qt, 0, 256, qt == 1
                    else:
                        li = qt - 2
                        slot, t0g, width, last = li % 4, NCTX + (li // 4) * 512, 512, li % 4 == 3
                    stg = stage[nst % 2]
                    s.cp(stg[:, :, slot * 128:(slot + 1) * 128], pt[:, 0:256].rearrange("p (j t) -> p j t", j=2), eng="act")
                    if last:
                        s.dma(YT[hk * 256:(hk + 1) * 256, t0g:t0g + width].rearrange("(j p) t -> p j t", p=128), stg[:, :, :width], queue=STQ)
                        nst += 1
            s.flush()

    def phase_na(self, l, ctx_out):
        s = self.s
        NB = NA_NB
        with ExitStack() as ph:
            masks = s.sb([128, NB, 128], BF16, stack=ph)
            ebm = s.sb([128, 2, NB, 128], BF16, stack=ph)
            bg = s.sb([128, 2, 7, 128], F32, stack=ph)
            eb = s.sb([128, 2, 7, 128], BF16, stack=ph)
            QT = s.sb([128, T], BF16, stack=ph)
            KT = s.sb([128, T], BF16, stack=ph)
            Vraw = s.sb([128, 34, 128], BF16, stack=ph)
            Vaug = s.sb([128, 34, 2, 65], BF16, stack=ph)
            E = [s.sb([128, 7, 128], BF16, stack=ph) for _ in range(2)]
            yb = [s.sb([128, 2, 64], BF16, stack=ph) for _ in range(2)]
            den = [s.sb([128, 2], F32, stack=ph) for _ in range(2)]
            stage = [s.sb([128, 512], BF16, stack=ph) for _ in range(2)]
            pS = [s.ps([128, 512], F32, stack=ph) for _ in range(4)]
            pO = [s.ps([128, 512], F32, stack=ph) for _ in range(2)]
            pT = [s.ps([128, 1024], BF16, stack=ph) for _ in range(2)]
            s.dma(masks[:], self.na_masks)
            YT = self.YT[1]
            qtiles = ([0, 1] if ctx_out else []) + list(range(2, 34))
            n = 0
            nps = 0
            nst = 0
            for blk in range(4):
                s.dma(QT[:], self.NQT[blk * 128:(blk + 1) * 128, :])
                s.dma(KT[:], self.NKT[blk * 128:(blk + 1) * 128, :])
                s.dma(Vraw[:], self.NV[:, blk * 128:(blk + 1) * 128].rearrange("(a p) c -> p a c", p=128))
                s.memset(Vaug[:], 1.0)
                s.cp(Vaug[:, :, :, 0:64], Vraw[:].rearrange("p a (h d) -> p a h d", h=2))
                s.dma(bg[:], self.na_bias[l, blk])
                s.act(eb[:], bg[:], AF.Exp)
                for hh in range(2):
                    for bi in range(NB):
                        s.tt(ebm[:, hh, bi, :], eb[:, hh, NA_BLOCK_DI[bi], :], masks[:, bi, :], ALU.mult)
                for qi, qt in enumerate(qtiles):
                    po = pO[n % 2]
                    y_ = yb[n % 2]
                    dn = den[n % 2]
                    n += 1
                    for hh in range(2):
                        base = 64 * hh
                        if qt < 2:
                            keys, off, nlat = [0, 1], 0, 0
                        else:
                            off, kts = NA_CFG[qt - 2]
                            nlat = len(kts)
                            keys = [2 + k for k in kts] + [0, 1]
                        e = E[nps % 2]
                        p0, p1 = pS[(2 * nps) % 4], pS[(2 * nps + 1) % 4]
                        nps += 1
                        nk = len(keys)
                        for ki, kt in enumerate(keys):
                            p = p0 if ki < 4 else p1
                            s.mm(p[:, (ki % 4) * 128:(ki % 4 + 1) * 128], KT[base:base + 64, kt * 128:(kt + 1) * 128],
                                 QT[base:base + 64, qt * 128:(qt + 1) * 128])
                        n0 = min(nk, 4)
                        s.act(e[:, 0:n0, :], p0[:, :n0 * 128].rearrange("p (k q) -> p k q", q=128), AF.Exp, scale=0.125)
                        if nk > 4:
                            s.act(e[:, 4:nk, :], p1[:, :(nk - 4) * 128].rearrange("p (k q) -> p k q", q=128), AF.Exp, scale=0.125)
                        if nlat:
                            s.tt(e[:, 0:nlat, :], e[:, 0:nlat, :], ebm[:, hh, off:off + nlat, :], ALU.mult)
                        for ki, kt in enumerate(keys):
                            s.mm(po[:, hh * 65:(hh + 1) * 65], e[:, ki, :], Vaug[:, kt, hh, :], start=(ki == 0), stop=(ki == nk - 1))
                    pov = po[:, 0:130].rearrange("p (g c) -> p g c", c=65)
                    s.cp(dn[:], pov[:, :, 64])
                    s.recip(dn[:], dn[:])
                    s.tt(y_[:], pov[:, :, 0:64], dn[:].unsqueeze(2).to_broadcast([128, 2, 64]), ALU.mult)
                    pt = pT[n % 2]
                    s.tr(pt[:, 0:128], y_[:].rearrange("p g d -> p (g d)"), self.ident[:])
                    if qt < 2:
                        slot, t0g, width, last = qt, 0, 256, qt == 1
                    else:
                        li = qt - 2
                        slot, t0g, width, last = li % 4, NCTX + (li // 4) * 512, 512, li % 4 == 3
                    stg = stage[nst % 2]
                    s.cp(stg[:, slot * 128:(slot + 1) * 128], pt[:, 0:128], eng="act")
                    if last:
                        s.dma(YT[blk * 128:(blk + 1) * 128, t0g:t0g + width], stg[:, :width], queue=STQ)
                        nst += 1
            s.flush()

    def phase_ssd(self, l, ctx_out):
        s = self.s
        NP = T + 6
        NO = T + 2

        def i0_of(a):
            return a * 128 if a < 2 else 258 + (a - 2) * 128

        with ExitStack() as ph:
            xp = [s.sb([128, NP], F32, stack=ph) for _ in range(2)]
            acc = s.sb([128, NO], F32, stack=ph)
            u = s.sb([128, NO], F32, stack=ph)
            ub = s.sb([128, NO], BF16, stack=ph)
            stf = s.sb([128, 34, 128], F32, stack=ph)
            stb = s.sb([128, 34, 128], BF16, stack=ph)
            cw = s.sb([128, 8, 5], F32, stack=ph)
            cb = s.sb([128, 8], F32, stack=ph)
            pt = [s.ps([128, 512], F32, stack=ph) for _ in range(4)]
            identf = s.sb([128, 128], F32, stack=ph)
            s.dma(identf[:], self.ident_in)
            s.dma(cw[:], self.ssd_convw_col[:, l])
            s.dma(cb[:], self.ssd_convb_col[:, l])
            for b in range(2):
                s.memset(xp[b][:, 0:2], 0.0)
                s.memset(xp[b][:, 258:260], 0.0)
                s.memset(xp[b][:, NP - 2:NP], 0.0)
            npt = 0
            for blk in range(8):
                x_ = xp[blk % 2]
                s.dma(x_[:, 2:258], self.XBCT[blk * 128:(blk + 1) * 128, 0:256])
                s.dma(x_[:, 260:260 + NLAT], self.XBCT[blk * 128:(blk + 1) * 128, 256:T])
                s.ts(acc[:], x_[:, 0:NO], cw[:, blk, 0:1], None, op0=ALU.mult)
                for k in range(1, 5):
                    s.stt(acc[:], x_[:, k:k + NO], cw[:, blk, k:k + 1], acc[:], ALU.mult, ALU.add)
                s.act(u[:], acc[:], AF.Silu, bias=cb[:, blk:blk + 1])
                if blk >= 4:
                    s.cp(ub[:], u[:], eng="pool")
                    s.dma(self.BCT[(blk - 4) * 128:(blk - 3) * 128, 0:256], ub[:, 0:256], queue=STQ)
                    s.dma(self.BCT[(blk - 4) * 128:(blk - 3) * 128, 256:T], ub[:, 258:258 + NLAT], queue=STQ)
                if blk < 6:
                    for a4 in range(0, 34, 4):
                        p = pt[npt % 4]
                        npt += 1
                        na = min(4, 34 - a4)
                        for j in range(na):
                            i0 = i0_of(a4 + j)
                            s.mm(p[:, j * 128:(j + 1) * 128], u[:, i0:i0 + 128], identf[:])
                        pv = p[:, :na * 128].rearrange("p (a c) -> p a c", c=128)
                        if blk < 4:
                            s.cp(stf[:, a4:a4 + na, :], pv, eng="act")
                        s.cp(stb[:, a4:a4 + na, :], pv, eng="dve")
                    if blk < 4:
                        s.dma(self.XTM[:, blk * 128:(blk + 1) * 128].rearrange("(a p) c -> p a c", p=128), stf[:], queue=STQ)
                        s.dma(self.XTMB[:, blk * 128:(blk + 1) * 128].rearrange("(a p) c -> p a c", p=128), stb[:], queue=STQ)
                    else:
                        s.dma(self.BTM[:, (blk - 4) * 128:(blk - 3) * 128].rearrange("(a p) c -> p a c", p=128), stb[:], queue=STQ)
            s.flush()

        with ExitStack() as ph:
            dt_all = s.sb([128, 34, 16], F32, stack=ph)
            dA_all = s.sb([128, 34, 16], F32, stack=ph)
            bias16 = s.sb([128, 16], F32, stack=ph)
            a16 = s.sb([128, 16], F32, stack=ph)
            dsk = s.sb([128, 8], F32, stack=ph)
            normg = s.sb([128, 512], F32, stack=ph)
            tri = [s.sb([128, 128], F32, stack=ph) for _ in range(2)]
            xb = s.sb([128, 34, 512], BF16, stack=ph)
            Bt = s.sb([128, 34, 256], BF16, stack=ph)
            BCT = s.sb([128, 4, T], BF16, stack=ph)
            H = s.sb([128, 8, 64], F32, stack=ph)
            Hb = s.sb([128, 8, 64], BF16, stack=ph)
            rhsA = s.sb([128, 8, 128], F32, stack=ph)
            MD = s.sb([128, 8, 128], F32, stack=ph)
            S_ = s.sb([128, 8, 128], F32, stack=ph)
            Wt = s.sb([128, 8, 128], BF16, stack=ph)
            acol = s.sb([128, 8], F32, stack=ph)
            ea = s.sb([128, 8], F32, stack=ph)
            de = s.sb([128, 8], F32, stack=ph)
            dec = s.sb([128, 8], F32, stack=ph)
            xw = s.sb([128, 8, 64], BF16, stack=ph)
            tmpy = s.sb([128, 8, 64], F32, stack=ph)
            yo = s.sb([128, 512], F32, stack=ph)
            xf = [s.sb([128, 512], F32, stack=ph) for _ in range(2)]
            yf = [s.sb([128, 512], F32, stack=ph) for _ in range(2)]
            zs = [s.sb([128, 512], F32, stack=ph) for _ in range(2)]
            gg = s.sb([128, 512], F32, stack=ph)
            junk = s.sb([128, 512], F32, stack=ph)
            ss = s.sb([128, 2], F32, stack=ph)
            ob = s.sb([128, 512], BF16, stack=ph)
            stage = [s.sb([128, 4, 512], BF16, stack=ph) for _ in range(2)]
            pAR = [s.ps([128, 512], F32, stack=ph) for _ in range(2)]
            pCB = s.ps([128, 512], F32, stack=ph)
            pY = s.ps([128, 512], F32, stack=ph)
            pYo = s.ps([128, 512], F32, stack=ph)
            pST = s.ps([128, 512], F32, stack=ph)
            pT = s.ps([128, 1024], BF16, stack=ph)
            s.dma(dt_all[:], self.DTR.rearrange("(a p) c -> p a c", p=128))
            s.dma(bias16[:], self.ssd_dt_bias[l].partition_broadcast(128))
            s.dma(a16[:], self.ssd_a_log[l].partition_broadcast(128))
            s.dma(dsk[:], self.ssd_d[l].partition_broadcast(128))
            s.dma(normg[:], self.ssd_norm_g[l].partition_broadcast(128))
            s.dma(tri[0][:], self.tri_f)
            s.dma(tri[1][:], self.tri_b)
            s.dma(xb[:], self.XTMB.rearrange("(a p) c -> p a c", p=128))
            s.dma(Bt[:], self.BTM.rearrange("(a p) c -> p a c", p=128))
            for j in range(4):
                s.dma(BCT[:, j, :], self.BCT[j * 128:(j + 1) * 128, :])
            s.tt(dt_all[:], dt_all[:], bias16[:].unsqueeze(1).to_broadcast([128, 34, 16]), ALU.add)
            s.act(dt_all[:], dt_all[:], AF.Exp)
            s.act(dt_all[:], dt_all[:], AF.Ln, bias=1.0)
            s.act(a16[:], a16[:], AF.Exp)
            s.ts(a16[:], a16[:], -1.0, None, op0=ALU.mult)
            s.tt(dA_all[:], dt_all[:], a16[:].unsqueeze(1).to_broadcast([128, 34, 16]), ALU.mult)
            nld = 0
            for d in range(2):
                order = [0, 1] + list(range(2, 34)) if d == 0 else [1, 0] + list(range(33, 1, -1))
                TR = tri[d]
                last_col = 127 if d == 0 else 0
                s.memset(H[:], 0.0)
                s.memset(Hb[:], 0.0)
                nst = 0
                for a in order:
                    want_y = ctx_out or a >= 2
                    cs_ = slice(a * 128, (a + 1) * 128)
                    dA = dA_all[:, a, d * 8:(d + 1) * 8]
                    dt = dt_all[:, a, d * 8:(d + 1) * 8]
                    s.tt(rhsA[:], TR[:].unsqueeze(1).to_broadcast([128, 8, 128]), dA.unsqueeze(2).to_broadcast([128, 8, 128]), ALU.mult)
                    for g in range(2):
                        s.mm(pAR[g][:], self.ones_f[:], rhsA[:, 4 * g:4 * g + 4, :].rearrange("p h i -> p (h i)"))
                    s.mm(pCB[:, 256:264], TR[:], dA)
                    s.cp(acol[:], pCB[:, 256:264])
                    if want_y:
                        for g in range(2):
                            s.mm(pCB[:, g * 128:(g + 1) * 128], BCT[:, g, cs_], BCT[:, 2 + g, cs_])
                        s.tt(MD[:], TR[:].unsqueeze(1).to_broadcast([128, 8, 128]), dt.unsqueeze(2).to_broadcast([128, 8, 128]), ALU.mult, eng="pool")
                        for g in range(2):
                            hs = slice(4 * g, 4 * g + 4)
                            s.tt(S_[:, hs, :], pAR[g][:].rearrange("p (h i) -> p h i", h=4),
                                 acol[:, hs].unsqueeze(2).to_broadcast([128, 4, 128]), ALU.subtract)
                            s.ts(S_[:, hs, :], S_[:, hs, :], 0.0, None, op0=ALU.min)
                            s.act(S_[:, hs, :], S_[:, hs, :], AF.Exp)
                            s.tt(S_[:, hs, :], S_[:, hs, :], MD[:, hs, :], ALU.mult)
                            s.tt(Wt[:, hs, :], S_[:, hs, :], pCB[:, g * 128:(g + 1) * 128].unsqueeze(1).to_broadcast([128, 4, 128]), ALU.mult)
                        for h in range(8):
                            s.mm(pY[:, h * 64:(h + 1) * 64], Wt[:, h, :], xb[:, a, h * 64:(h + 1) * 64])
                        for g in range(2):
                            s.mm(pYo[:, g * 256:(g + 1) * 256], BCT[:, 2 + g, cs_], Hb[:, 4 * g:4 * g + 4, :].rearrange("p h d -> p (h d)"))
                        s.act(ea[:], acol[:], AF.Exp)
                        s.tt(tmpy[:], pYo[:].rearrange("p (h d) -> p h d", h=8), ea[:].unsqueeze(2).to_broadcast([128, 8, 64]), ALU.mult)
                        s.tt(yo[:], tmpy[:].rearrange("p h d -> p (h d)"), pY[:], ALU.add)
                        if d == 0:
                            x_ = xf[nld % 2]
                            nld += 1
                            s.dma(x_[:], self.XTM[cs_, :])
                            s.tt(x_[:].rearrange("p (h d) -> p h d", h=8), x_[:].rearrange("p (h d) -> p h d", h=8),
                                 dsk[:].unsqueeze(2).to_broadcast([128, 8, 64]), ALU.mult, eng="pool")
                            s.tt(x_[:], x_[:], yo[:], ALU.add)
                            s.dma(self.YF[cs_, :], x_[:], queue=STQ)
                        else:
                            y_ = yf[nld % 2]
                            z_ = zs[nld % 2]
                            nld += 1
                            s.dma(y_[:], self.YF[cs_, :])
                            s.dma(z_[:], self.ZS[cs_, :])
                            s.tt(y_[:], y_[:], yo[:], ALU.add)
                            s.tt(gg[:], y_[:], z_[:], ALU.mult)
                            s.memset(ss[:], 0.0)
                            s.act(junk[:], gg[:], AF.Square, accum_out=ss[:, 0:1])
                            s.ts(ss[:, 1:2], ss[:, 0:1], 1.0 / 512, LN_EPS, op0=ALU.mult, op1=ALU.add)
                            s.sqrt(ss[:, 1:2], ss[:, 1:2])
                            s.recip(ss[:, 1:2], ss[:, 1:2])
                            s.stt(ob[:], gg[:], ss[:, 1:2], normg[:], ALU.mult, ALU.mult)
                            for j in range(4):
                                s.tr(pT[:, j * 128:(j + 1) * 128], ob[:, j * 128:(j + 1) * 128], self.ident[:])
                            if a < 2:
                                slot, t0g, width, lastt = a, 0, 256, a == 0
                            else:
                                li = a - 2
                                slot, t0g, width, lastt = li % 4, NCTX + (li // 4) * 512, 512, li % 4 == 0
                            stg = stage[nst % 2]
                            s.cp(stg[:, :, slot * 128:(slot + 1) * 128], pT[:, 0:512].rearrange("p (j t) -> p j t", j=4), eng="act")
                            if lastt:
                                s.dma(self.YT[0][:, t0g:t0g + width].rearrange("(j p) t -> p j t", p=128), stg[:, :, :width], queue=STQ)
                                nst += 1
                    tot = [pAR[g][:, last_col:512:128] for g in range(2)]
                    for g in range(2):
                        s.tt(de[:, 4 * g:4 * g + 4], tot[g], acol[:, 4 * g:4 * g + 4], ALU.subtract)
                        s.cp(dec[:, 4 * g:4 * g + 4], tot[g])
                    s.act(de[:], de[:], AF.Exp)
                    s.act(dec[:], dec[:], AF.Exp)
                    s.tt(de[:], de[:], dt, ALU.mult)
                    s.tt(xw[:], xb[:, a, :].rearrange("p (h d) -> p h d", h=8), de[:].unsqueeze(2).to_broadcast([128, 8, 64]), ALU.mult)
                    for h in range(8):
                        g = h // 4
                        s.mm(pST[:, h * 64:(h + 1) * 64], Bt[:, a, g * 128:(g + 1) * 128], xw[:, h, :])
                    s.tt(H[:], H[:], dec[:].unsqueeze(2).to_broadcast([128, 8, 64]), ALU.mult)
                    s.tt(H[:], H[:], pST[:].rearrange("p (h d) -> p h d", h=8), ALU.add)
                    s.cp(Hb[:], H[:], eng="act")
            s.flush()

    def phase_merge_a(self, l, ctx_out):
        s = self.s
        with ExitStack() as ph:
            wbr = [s.sb([128, 16, 512], BF16, stack=ph) for _ in range(2)]
            yt = [s.sb([128, 16, 512], BF16, stack=ph) for _ in range(2)]
            gt = [s.sb([128, 4, 512], BF16, stack=ph) for _ in range(6)]
            accf = [s.sb([128, 512], F32, stack=ph) for _ in range(2)]
            tmp = [s.sb([128, 512], F32, stack=ph) for _ in range(3)]
            accT = [s.sb([128, 4, 512], BF16, stack=ph) for _ in range(2)]
            pm = [s.ps([128, 512], F32, stack=ph) for _ in range(6)]
            WBR = self.WBR[l].rearrange("(j p) n -> p j n", p=128)
            GTv = self.GT.rearrange("(b c p) t -> p b c t", b=4, p=128)
            nw = ng = npm = nt3 = 0
            for ti, (t0, nt) in enumerate(TILES):
                if ti == 0 and not ctx_out:
                    continue
                y = yt[ti % 2]
                for br in range(4):
                    s.dma(y[:, br * 4:(br + 1) * 4, :nt], self.YT[br][:, t0:t0 + nt].rearrange("(k p) t -> p k t", p=128))
                for c4 in range(4):
                    w = wbr[nw % 2]
                    nw += 1
                    s.dma(w[:], WBR[:, :, c4 * 512:(c4 + 1) * 512])
                    ao = accT[c4 % 2]
                    for cc in range(4):
                        c = c4 * 4 + cc
                        g_ = gt[ng % 6]
                        ng += 1
                        s.dma(g_[:, :, :nt], GTv[:, :, c, t0:t0 + nt])
                        af = accf[c % 2]
                        for br in range(4):
                            p = pm[npm % 6]
                            npm += 1
                            for k in range(4):
                                s.mm(p[:, :nt], w[:, br * 4 + k, cc * 128:(cc + 1) * 128], y[:, br * 4 + k, :nt], start=(k == 0), stop=(k == 3))
                            if br == 0:
                                s.tt(af[:, :nt], p[:, :nt], g_[:, 0, :nt], ALU.mult)
                            else:
                                t_ = tmp[nt3 % 3]
                                nt3 += 1
                                s.tt(t_[:, :nt], p[:, :nt], g_[:, br, :nt], ALU.mult)
                                if br < 3:
                                    s.tt(af[:, :nt], af[:, :nt], t_[:, :nt], ALU.add, eng="pool")
                                else:
                                    s.tt(ao[:, cc, :nt], af[:, :nt], t_[:, :nt], ALU.add, eng="pool")
                    s.dma(self.ACCT[c4 * 512:(c4 + 1) * 512, t0:t0 + nt].rearrange("(j p) t -> p j t", p=128), ao[:, :, :nt], queue=STQ)
            s.flush()

    def _bc(self, tile_, vec):
        self.s.dma(tile_[:], vec.partition_broadcast(128))

    def ln_affine(self, u, junk, st, lng, lnb):
        s = self.s
        s.memset(st[:], 0.0)
        s.act(junk[:], u[:], AF.Identity, accum_out=st[:, 0:1])
        s.act(junk[:], u[:], AF.Square, accum_out=st[:, 1:2])
        s.ts(st[:, 2:3], st[:, 0:1], 1.0 / D, None, op0=ALU.mult)
        s.ts(st[:, 3:4], st[:, 1:2], 1.0 / D, None, op0=ALU.mult)
        s.tt(st[:, 4:5], st[:, 2:3], st[:, 2:3], ALU.mult)
        s.tt(st[:, 5:6], st[:, 3:4], st[:, 4:5], ALU.subtract)
        s.ts(st[:, 6:7], st[:, 5:6], LN_EPS, None, op0=ALU.add)
        s.sqrt(st[:, 6:7], st[:, 6:7])
        s.recip(st[:, 6:7], st[:, 6:7])
        s.stt(st[:, 7:8], st[:, 2:3], -1.0, st[:, 6:7], ALU.mult, ALU.mult)
        s.act(u[:], u[:], AF.Identity, bias=st[:, 7:8], scale=st[:, 6:7])
        s.tt(u[:], u[:], lng[:], ALU.mult)
        s.tt(u[:], u[:], lnb[:], ALU.add, eng="pool")

    def phase_merge_b(self, l, ctx_out, src):
        s = self.s
        with ExitStack() as ph:
            wout = s.sb([128, KC, D], BF16, stack=ph)
            acc = [s.sb([128, KC, 512], BF16, stack=ph) for _ in range(2)]
            xt = [s.sb([128, D], F32, stack=ph) for _ in range(2)]
            ut = [s.sb([128, D], F32, stack=ph) for _ in range(2)]
            g1 = s.sb([128, D], F32, stack=ph)
            lng = s.sb([128, D], F32, stack=ph)
            lnb = s.sb([128, D], F32, stack=ph)
            yb = [s.sb([128, D], BF16, stack=ph) for _ in range(2)]
            junk = s.sb([128, D], BF16, stack=ph)
            st = [s.sb([128, 8], F32, stack=ph) for _ in range(4)]
            tmp8 = s.sb([128, 8, 128], F32, stack=ph)
            h2 = [s.sb([128, KC, 512], BF16, stack=ph) for _ in range(2)]
            tp = [s.ps([128, 1024], BF16, stack=ph) for _ in range(2)]
            pm = [s.ps([128, 512], F32, stack=ph) for _ in range(4)]
            s.dma(wout[:], self.WOUT[l].rearrange("(k p) n -> p k n", p=128))
            self._bc(lng, self.ln1_g[l])
            self._bc(lnb, self.ln1_b[l])
            npm = nx = 0
            cur_m = None
            for ti, (t0, nt) in enumerate(TILES):
                if ti == 0 and not ctx_out:
                    continue
                m = 1 if ti == 0 else 0
                if m != cur_m:
                    self._bc(g1, self.MODR[l, 0, m])
                    cur_m = m
                a_ = acc[ti % 2]
                h_ = h2[ti % 2]
                s.dma(a_[:, :, :nt], self.ACCT[:, t0:t0 + nt].rearrange("(k p) t -> p k t", p=128))
                sc = self.modc[:, l, 48:64, m]
                sh = self.modc[:, l, 32:48, m]
                for a in range(nt // 128):
                    x_, u_, y_ = xt[nx % 2], ut[nx % 2], yb[nx % 2]
                    st1, st2 = st[(2 * nx) % 4], st[(2 * nx + 1) % 4]
                    nx += 1
                    r0 = t0 + a * 128
                    s.dma(x_[:], src[r0:r0 + 128, :])
                    for cb in range(4):
                        p = pm[npm % 4]
                        npm += 1
                        for k in range(KC):
                            s.mm(p[:], a_[:, k, a * 128:(a + 1) * 128], wout[:, k, cb * 512:(cb + 1) * 512], start=(k == 0), stop=(k == KC - 1))
                        s.tt(u_[:, cb * 512:(cb + 1) * 512], p[:], g1[:, cb * 512:(cb + 1) * 512], ALU.mult)
                    s.stt(u_[:], x_[:], ALPHA, u_[:], ALU.mult, ALU.add)
                    self.ln_affine(u_, junk, st1, lng, lnb)
                    s.dma(self.X1[r0:r0 + 128, :], u_[:], queue=STQ)
                    self.ln_rows(u_, y_, junk, st2)
                    self.mod_transpose(y_, h_, a * 128, tp, tmp8, sc, sh)
                s.dma(self.H2T[:, t0:t0 + nt].rearrange("(k p) t -> p k t", p=128), h_[:, :, :nt], queue=STQ)
            s.flush()

    def phase_ffn(self, l, ctx_out, dst, dst_off):
        s = self.s
        with ExitStack() as ph:
            hid = s.sb([128, FC, 512], BF16, stack=ph)
            wdn = [s.sb([128, 11, 512], BF16, stack=ph) for _ in range(2)]
            uall = s.sb([128, 4, D], F32, stack=ph)
            wg = [s.sb([128, KC, 256], BF16, stack=ph) for _ in range(2)]
            wu = [s.sb([128, KC, 256], BF16, stack=ph) for _ in range(2)]
            h2 = s.sb([128, KC, 514], BF16, stack=ph)
            gb = [s.sb([128, 514], F32, stack=ph) for _ in range(2)]
            ac = [s.sb([128, 512], F32, stack=ph) for _ in range(2)]
            sg = [s.sb([128, 512], F32, stack=ph) for _ in range(2)]
            xt = [s.sb([128, D], F32, stack=ph) for _ in range(2)]
            g2 = s.sb([128, D], F32, stack=ph)
            lng = s.sb([128, D], F32, stack=ph)
            lnb = s.sb([128, D], F32, stack=ph)
            junk = s.sb([128, D], BF16, stack=ph)
            st = [s.sb([128, 8], F32, stack=ph) for _ in range(2)]
            cw = s.sb([128, FC, 3], F32, stack=ph)
            cbias = s.sb([128, FC], F32, stack=ph)
            pb = [s.ps([128, 512], F32, stack=ph) for _ in range(8)]
            pg, pu, pd = pb[0:2], pb[2:4], pb[4:8]
            s.dma(cw[:], self.ffn_convw_col[:, l])
            s.dma(cbias[:], self.ffn_convb_col[:, l])
            self._bc(lng, self.ln2_g[l])
            self._bc(lnb, self.ln2_b[l])
            WUP = self.WUP[l].rearrange("(k p) n -> p k n", p=128)
            WDN = self.WDN[l].rearrange("(f p) n -> p f n", p=128)
            nwg = nfc = nwd = nx = 0
            cur_m = None
            for ti, (t0, nt) in enumerate(TILES):
                if ti == 0 and not ctx_out:
                    continue
                m = 1 if ti == 0 else 0
                if m != cur_m:
                    self._bc(g2, self.MODR[l, 1, m])
                    cur_m = m
                seq_lo, seq_hi = (0, NCTX) if ti == 0 else (NCTX, T)
                left_pad = t0 == seq_lo
                right_pad = t0 + nt == seq_hi
                lo = t0 - (0 if left_pad else 1)
                hi = t0 + nt + (0 if right_pad else 1)
                if left_pad:
                    s.memset(h2[:, :, 0:1], 0.0)
                if right_pad:
                    s.memset(h2[:, :, nt + 1:nt + 2], 0.0)
                s.dma(h2[:, :, (1 if left_pad else 0):(1 if left_pad else 0) + hi - lo], self.H2T[:, lo:hi].rearrange("(k p) t -> p k t", p=128))
                for f2 in range(FC // 2):
                    wg_, wu_ = wg[nwg % 2], wu[nwg % 2]
                    nwg += 1
                    s.dma(wg_[:], WUP[:, :, f2 * 256:(f2 + 1) * 256])
                    s.dma(wu_[:], WUP[:, :, D_FF + f2 * 256:D_FF + (f2 + 1) * 256])
                    for ff in range(2):
                        fc = f2 * 2 + ff
                        pg_, pu_ = pg[nfc % 2], pu[nfc % 2]
                        ph_ = pb[4][:, (nfc % 2) * 2:(nfc % 2) * 2 + 2]
                        gb_, ac_, sg_ = gb[nfc % 2], ac[nfc % 2], sg[nfc % 2]
                        nfc += 1
                        for k in range(KC):
                            s.mm(pg_[:, :nt], wg_[:, k, ff * 128:(ff + 1) * 128], h2[:, k, 1:nt + 1], start=(k == 0), stop=(k == KC - 1))
                        for k in range(KC):
                            s.mm(ph_, wg_[:, k, ff * 128:(ff + 1) * 128], h2[:, k, 0:nt + 2:nt + 1], start=(k == 0), stop=(k == KC - 1))
                        for k in range(KC):
                            s.mm(pu_[:, :nt], wu_[:, k, ff * 128:(ff + 1) * 128], h2[:, k, 1:nt + 1], start=(k == 0), stop=(k == KC - 1))
                        s.cp(gb_[:, 1:nt + 1], pg_[:, :nt], eng="act")
                        s.cp(gb_[:, 0:nt + 2:nt + 1], ph_, eng="act")
                        if left_pad:
                            s.memset(gb_[:, 0:1], 0.0, eng="pool")
                        if right_pad:
                            s.memset(gb_[:, nt + 1:nt + 2], 0.0, eng="pool")
                        s.ts(ac_[:, :nt], gb_[:, 0:nt], cw[:, fc, 0:1], None, op0=ALU.mult)
                        s.stt(ac_[:, :nt], gb_[:, 1:nt + 1], cw[:, fc, 1:2], ac_[:, :nt], ALU.mult, ALU.add)
                        s.stt(ac_[:, :nt], gb_[:, 2:nt + 2], cw[:, fc, 2:3], ac_[:, :nt], ALU.mult, ALU.add)
                        s.act(sg_[:, :nt], ac_[:, :nt], AF.Silu, bias=cbias[:, fc:fc + 1])
                        s.tt(hid[:, fc, :nt], sg_[:, :nt], pu_[:, :nt], ALU.mult)
                na = nt // 128
                for cb in range(4):
                    for q4 in range(4):
                        w = wdn[nwd % 2]
                        nwd += 1
                        s.dma(w[:], WDN[:, q4 * 11:(q4 + 1) * 11, cb * 512:(cb + 1) * 512])
                        for a in range(na):
                            for f in range(11):
                                fc = q4 * 11 + f
                                s.mm(pd[a][:], hid[:, fc, a * 128:(a + 1) * 128], w[:, f, :], start=(fc == 0), stop=(fc == FC - 1))
                    for a in range(na):
                        s.tt(uall[:, a, cb * 512:(cb + 1) * 512], pd[a][:], g2[:, cb * 512:(cb + 1) * 512], ALU.mult)
                for a in range(na):
                    x_ = xt[nx % 2]
                    st_ = st[nx % 2]
                    u_ = uall[:, a, :]
                    nx += 1
                    r0 = t0 + a * 128
                    s.dma(x_[:], self.X1[r0:r0 + 128, :])
                    s.stt(u_, x_[:], ALPHA, u_, ALU.mult, ALU.add)
                    self.ln_affine(u_, junk, st_, lng, lnb)
                    if r0 - dst_off >= 0:
                        s.dma(dst[r0 - dst_off:r0 - dst_off + 128, :], u_, queue=STQ)
            s.flush()

    def build(self):
        self.declare()
        if self.only is not None:
            self.phase0(light=True)
            for (name, l, ctx_out) in self.only:
                getattr(self, "phase_" + name)(l, ctx_out)
            return self.finish()
        self.phase0()
        if self.stop_after == "phase0":
            return self.finish()
        for l in self.layers:
            ctx_out = l < DEPTH - 1
            src = self.xin if l == 0 else self.XS
            self.phase_inproj(l, src)
            if self.stop_after == f"inproj{l}":
                return self.finish()
            self.phase_ssd(l, ctx_out)
            self.phase_na(l, ctx_out)
            self.phase_swa(l, ctx_out)
            self.phase_fnet(l, ctx_out)
            if self.stop_after == f"mix{l}":
                return self.finish()
            self.phase_merge_a(l, ctx_out)
            self.phase_merge_b(l, ctx_out, src)
            if self.stop_after == f"merge{l}":
                return self.finish()
            if l < DEPTH - 1:
                self.phase_ffn(l, ctx_out, self.XS, 0)
            else:
                self.phase_ffn(l, ctx_out, self.out, NCTX)
            if self.stop_after == f"ffn{l}":
                return self.finish()
        return self.finish()

    def finish(self):
        self.s.flush(final=True)
        self.stack.close()
        return self.nc


def host_inputs(inputs):
    f32 = np.float32
    shared = {}
    shared["w_ada"] = np.ascontiguousarray(inputs["w_ada"], dtype=f32)
    shared["b_ada"] = np.ascontiguousarray(inputs["b_ada"], dtype=f32)
    shared["b_ada_col"] = np.ascontiguousarray(inputs["b_ada"].reshape(DEPTH, 96, 128).transpose(2, 0, 1), dtype=f32)
    shared["w_in_r"] = np.ascontiguousarray(inputs["w_in"][:, :, W_PERM], dtype=f32)
    shared["b_gate_col"] = np.ascontiguousarray(inputs["b_gate"].reshape(DEPTH, 64, 128).transpose(2, 0, 1), dtype=f32)
    shared["ident"] = np.eye(128, dtype=f32)
    t = np.arange(NLAT)
    rows = (t // GRID_W).astype(f32)
    cols = (t % GRID_W).astype(f32)
    nf = 16
    inv = (10000.0 ** (-np.arange(nf, dtype=f32) / nf)).astype(f32)
    ang = np.stack([rows[:, None] * inv, cols[:, None] * inv], axis=1).astype(f32)
    cosv, sinv = np.cos(ang).astype(f32), np.sin(ang).astype(f32)
    rc = np.zeros((128, NLAT), f32)
    rs = np.zeros((128, NLAT), f32)
    for r in range(128):
        d = r % 64
        ax, half, f = d // 32, (d % 32) // 16, d % 16
        rc[r] = cosv[:, ax, f]
        rs[r] = sinv[:, ax, f] * (-1.0 if half == 0 else 1.0)
    shared["rope_cos"] = rc
    shared["rope_sin"] = rs
    bf = ml_dtypes.bfloat16
    shared["w_branch"] = np.ascontiguousarray(inputs["w_branch"].reshape(DEPTH, 4 * 512, D), dtype=f32)
    shared["w_out"] = np.ascontiguousarray(inputs["w_out"], dtype=f32)
    shared["ffn_w_up"] = np.ascontiguousarray(inputs["ffn_w_up"], dtype=f32)
    shared["ffn_w_down"] = np.ascontiguousarray(inputs["ffn_w_down"], dtype=f32)
    for k in ("ln1_g", "ln1_b", "ln2_g", "ln2_b"):
        shared[k] = np.ascontiguousarray(inputs[k], dtype=f32)
    shared["ffn_convw_col"] = np.ascontiguousarray(inputs["ffn_conv_w"].reshape(DEPTH, 3, FC, 128).transpose(3, 0, 2, 1), dtype=f32)
    shared["ffn_convb_col"] = np.ascontiguousarray(inputs["ffn_conv_b"].reshape(DEPTH, FC, 128).transpose(2, 0, 1), dtype=f32)
    shared["ssd_convw_col"] = np.ascontiguousarray(inputs["ssd_conv_w"].reshape(DEPTH, 5, 8, 128).transpose(3, 0, 2, 1), dtype=f32)
    shared["ssd_convb_col"] = np.ascontiguousarray(inputs["ssd_conv_b"].reshape(DEPTH, 8, 128).transpose(2, 0, 1), dtype=f32)
    shared["ssd_dt_bias"] = np.ascontiguousarray(inputs["ssd_dt_bias"].reshape(DEPTH, 16), dtype=f32)
    shared["ssd_a_log"] = np.ascontiguousarray(inputs["ssd_a_log"].reshape(DEPTH, 16), dtype=f32)
    shared["ssd_d"] = np.ascontiguousarray(inputs["ssd_d"], dtype=f32)
    shared["ssd_norm_g"] = np.ascontiguousarray(inputs["ssd_norm_g"], dtype=f32)
    jj, ii = np.arange(128)[:, None], np.arange(128)[None, :]
    shared["tri_f"] = (jj <= ii).astype(f32)
    shared["tri_b"] = (jj >= ii).astype(f32)
    k128 = np.arange(128, dtype=np.float64)
    a128 = 2 * np.pi * np.outer(k128, k128) / 128
    shared["cs128"] = (np.concatenate([np.cos(a128), np.sin(a128)], axis=1) / np.sqrt(128.0)).astype(bf)
    kN = np.arange(NLAT, dtype=np.int64)
    aN = 2 * np.pi * ((np.outer(kN, kN) % NLAT).astype(np.float64)) / NLAT
    shared["dftc"] = (np.cos(aN) / np.sqrt(float(NLAT))).astype(bf)
    shared["dfts"] = (-np.sin(aN) / np.sqrt(float(NLAT))).astype(bf)
    del aN
    kC = np.arange(NCTX, dtype=np.float64)
    aC = 2 * np.pi * np.outer(kC, kC) / NCTX
    shared["c256"] = (np.cos(aC) / np.sqrt(float(NCTX))).astype(bf)
    shared["s256"] = (-np.sin(aC) / np.sqrt(float(NCTX))).astype(bf)
    kk, qq = np.arange(128)[:, None], np.arange(128)[None, :]
    shared["band_masks"] = np.ascontiguousarray(np.stack([(qq <= kk), (kk <= qq)], axis=1).astype(np.float32)).astype(bf)
    shared["swa_sink"] = np.ascontiguousarray(inputs["swa_sink"], dtype=f32)
    shared["na_masks"] = np.ascontiguousarray(np.stack([b[1] for b in _NA_BLOCKS], axis=1).astype(np.float32)).astype(bf)
    dr, dc = _na_bias_index()
    rpb = np.asarray(inputs["na_rpb"], dtype=f32)
    g = rpb[:, :, dr, dc]
    g = g.reshape(DEPTH, 4, 2, 7, 128, 128).transpose(0, 1, 4, 2, 3, 5)
    shared["na_bias"] = np.ascontiguousarray(g, dtype=f32)
    per_core = []
    for b in range(NCORES):
        d = {}
        d["xin"] = np.ascontiguousarray(np.concatenate([inputs["ctx"][b], inputs["x"][b]], axis=0), dtype=f32)
        c2 = np.stack([inputs["c"][b], inputs["c_ctx"]], axis=-1)
        d["cT"] = np.ascontiguousarray(c2.reshape(KC, 128, 2).transpose(1, 0, 2), dtype=f32)
        per_core.append(d)
    return shared, per_core


ACTIVE = (0, 1, 4, 5)


def kernel(**inputs):
    shared, per_core = host_inputs(inputs)
    prog = Prog()
    nc = prog.build()
    real = [dict(shared, **pc) for pc in per_core]
    zero = {k: np.zeros_like(v) for k, v in real[0].items()}
    in_maps = [zero] * 8
    in_maps = list(in_maps)
    for b, c in enumerate(ACTIVE):
        in_maps[c] = real[b]
    res = run_bass_kernel_spmd(nc, in_maps, core_ids=list(range(8)))
    out = np.stack([np.asarray(res.results[c]["out"]) for c in ACTIVE], axis=0)
    return out.astype(np.float32)
```

```python
from contextlib import ExitStack
import math
import numpy as np
import ml_dtypes
import concourse.bass as bass
import concourse.mybir as mybir
from concourse.bass_utils import run_bass_kernel_spmd

F32 = mybir.dt.float32
BF16 = mybir.dt.bfloat16
AF = mybir.ActivationFunctionType
ALU = mybir.AluOpType
AX = mybir.AxisListType

D = 2048
NCTX = 256
NLAT = 4096
T = NCTX + NLAT
DEPTH = 2
KC = D // 128
GRID_W = 64
D_FF = 5632
FC = D_FF // 128
ALPHA = (2.0 * DEPTH) ** 0.25
LN_EPS = 1e-6
TILES = [(0, 256)] + [(256 + 512 * i, 512) for i in range(8)]
NCORES = 4
STQ = "act"

_Z0, _XBC0, _DT0, _NQ0, _NK0, _NV0, _SQ0, _SK0, _SV0, _FN0, _G0 = 0, 512, 1536, 1552, 2064, 2576, 3088, 3600, 3728, 3856, 4368


def _partner(d):
    ax, half, f = d // 32, (d % 32) // 16, d % 16
    return ax * 32 + (1 - half) * 16 + f


def _w_in_perm():
    cols = []
    cols += list(range(_XBC0, _XBC0 + 1024))
    cols += list(range(_NQ0, _NQ0 + 512))
    cols += list(range(_NK0, _NK0 + 512))
    for j in range(4):
        blk = list(range(_SQ0 + 128 * j, _SQ0 + 128 * (j + 1)))
        cols += blk
        cols += [_SQ0 + 128 * j + 64 * (i // 64) + _partner(i % 64) for i in range(128)]
    cols += list(range(_FN0, _FN0 + 512))
    cols += list(range(_G0, _G0 + 8192))
    cols += list(range(_SK0, _SK0 + 128))
    cols += [_SK0 + 64 * (i // 64) + _partner(i % 64) for i in range(128)]
    n_fm = len(cols)
    cols += list(range(_Z0, _Z0 + 512))
    cols += list(range(_NV0, _NV0 + 512))
    cols += list(range(_SV0, _SV0 + 128))
    cols += list(range(_DT0, _DT0 + 16))
    return np.array(cols), n_fm


W_PERM, N_FM = _w_in_perm()
N_IN = len(W_PERM)
assert N_FM == 12032 and N_IN == 13200
FM_GROUPS = ([("xbc", i) for i in range(2)] + [("nq", 0), ("nk", 0)] + [("rope", i) for i in range(2)]
             + [("fn", 0)] + [("gate", i) for i in range(16)] + [("ropek", 0)])


def _na_geometry():
    rows = NLAT // GRID_W
    cfg_map = {}
    blocks = []
    cfgs = []
    qrl = np.repeat(np.arange(2), 64)
    qc = np.tile(np.arange(64), 2)
    for qt in range(rows // 2):
        r = 2 * qt + qrl
        rstart = np.clip(r - 4, 0, rows - 8)
        wstart = np.clip(qc - 8, 0, GRID_W - 16)
        kt_lo = int(rstart.min()) // 2
        kt_hi = int(rstart.max() + 7) // 2
        entry = []
        for kt in range(kt_lo, kt_hi + 1):
            kr = 2 * kt + qrl
            kc = qc
            valid = ((kr[:, None] >= rstart[None, :]) & (kr[:, None] < rstart[None, :] + 8)
                     & (kc[:, None] >= wstart[None, :]) & (kc[:, None] < wstart[None, :] + 16))
            entry.append((kt, 2 * (kt - qt), valid))
        key = tuple((d, v.tobytes()) for _, d, v in entry)
        if key not in cfg_map:
            cfg_map[key] = len(blocks)
            for _, d, v in entry:
                blocks.append(((d + 6) // 2, v))
        cfgs.append((cfg_map[key], [kt for kt, _, _ in entry]))
    return cfgs, blocks


NA_CFG, _NA_BLOCKS = _na_geometry()
NA_NB = len(_NA_BLOCKS)
NA_BLOCK_DI = [b[0] for b in _NA_BLOCKS]


def _na_bias_index():
    qrl = np.repeat(np.arange(2), 64)
    qc = np.tile(np.arange(64), 2)
    dr = np.zeros((7, 128, 128), np.int64)
    dc = np.zeros((7, 128, 128), np.int64)
    for di in range(7):
        delta = 2 * di - 6
        dr[di] = np.clip(delta + qrl[:, None] - qrl[None, :] + 7, 0, 14)
        dc[di] = np.clip(qc[:, None] - qc[None, :] + 15, 0, 30)
    return dr, dc


class Sched:
    COMPUTE = ("pe", "act", "dve", "pool")
    NDMA = 12

    def __init__(self, nc, stack):
        self.nc = nc
        self.stack = stack
        self.eng = {"pe": nc.tensor, "act": nc.scalar, "dve": nc.vector, "pool": nc.gpsimd, "sp": nc.sync}
        self.ops = []
        self.buf = {}
        self.dbuf = {}
        self.dram_names = set()
        self.bg_ops = set()
        self.psum_names = set()
        self.nsb = 0
        self.emitted = 0
        self.sem = {e: stack.enter_context(nc.semaphore(f"s_{e}")) for e in self.COMPUTE}
        self.dsem = {q: [stack.enter_context(nc.semaphore(f"d_{q}_{j}")) for j in range(self.NDMA)]
                     for q in ("sp", "pool", "act")}
        self.tok = []
        self.extra = []
        self.cnt = {e: 0 for e in self.COMPUTE}
        self.dcnt = {q: 0 for q in self.dsem}
        self.duse = {q: [0] * self.NDMA for q in self.dsem}
        self.dprev = {q: [None] * self.NDMA for q in self.dsem}
        self.seen = {e: {} for e in self.eng}

    def dram(self, name, shape, dtype, kind="Internal"):
        self.dram_names.add(name)
        return self.nc.dram_tensor(name, list(shape), dtype, kind=kind).ap()

    def sb(self, shape, dtype, name=None, stack=None):
        self.nsb += 1
        name = (name or "sb") + f"_{self.nsb}"
        return (stack or self.stack).enter_context(self.nc.sbuf_tensor(name, list(shape), dtype))

    def ps(self, shape, dtype, name=None, stack=None):
        self.nsb += 1
        name = (name or "ps") + f"_{self.nsb}"
        t = (stack or self.stack).enter_context(self.nc.psum_tensor(name, list(shape), dtype))
        self.psum_names.add(t[:].tensor.name)
        return t

    def _key(self, ap):
        if isinstance(ap, (str, tuple)):
            return ap
        return ap.tensor.name

    def _add(self, eng, fn, reads, writes, is_dma):
        deps = set()
        rk = [self._key(a) for a in reads if a is not None and not isinstance(a, (int, float))]
        wk = [self._key(a) for a in writes]
        for k in rk:
            if k in self.dram_names:
                b = self.dbuf.get(k)
                if b:
                    deps.update(b[0])
            else:
                b = self.buf.get(k)
                if b is not None and b[0] is not None:
                    deps.add(b[0])
                if b is not None and k in self.psum_names:
                    deps.update(v for e2, v in b[1].items() if e2 != eng)
        for k in wk:
            if k in self.dram_names:
                b = self.dbuf.get(k)
                if b:
                    deps.update(b[1])
            else:
                b = self.buf.get(k)
                if b is not None:
                    if b[0] is not None:
                        deps.add(b[0])
                    deps.update(b[1].values())
                    deps.update(b[2])
        i = len(self.ops)
        self.ops.append([eng, fn, deps, is_dma])
        for k in rk:
            if k in self.dram_names:
                self.dbuf.setdefault(k, [[], []])[1].append(i)
            else:
                b = self.buf.setdefault(k, [None, {}, []])
                if is_dma:
                    b[2].append(i)
                else:
                    b[1][eng] = i
        for k in wk:
            if k in self.dram_names:
                self.dbuf.setdefault(k, [[], []])[0].append(i)
            else:
                self.buf[k] = [i, {}, []]
        return i

    def op(self, eng, fn, reads=(), writes=()):
        return self._add(eng, fn, reads, writes, False)

    def dma(self, out, in_, queue="sp", reads=None, writes=None, bg=False):
        i = self._add(queue, lambda e: e.dma_start(out=out, in_=in_),
                      [in_] if reads is None else reads, [out] if writes is None else writes, True)
        if bg:
            self.bg_ops.add(i)
        return i

    def mm(self, out, lhsT, rhs, start=True, stop=True):
        return self.op("pe", lambda e: e.matmul(out, lhsT, rhs, start=start, stop=stop), [lhsT, rhs], [out])

    def tr(self, out, in_, ident):
        return self.op("pe", lambda e: e.transpose(out, in_, ident), [in_, ident], [out])

    def act(self, out, in_, func, bias=None, scale=None, accum_out=None):
        kw = {}
        if bias is not None:
            kw["bias"] = bias
        if scale is not None:
            kw["scale"] = scale
        if accum_out is not None:
            kw["accum_out"] = accum_out
        rd = [in_] + [a for a in (bias, scale) if a is not None and not isinstance(a, (int, float))]
        wr = [out] + ([accum_out] if accum_out is not None else [])
        return self.op("act", lambda e: e.activation(out, in_, func, **kw), rd, wr)

    def tt(self, out, in0, in1, op, eng="dve"):
        return self.op(eng, lambda e: e.tensor_tensor(out, in0, in1, op), [in0, in1], [out])

    def ts(self, out, in0, s1, s2=None, op0=ALU.mult, op1=None, eng="dve"):
        kw = {}
        if op1 is not None:
            kw["op1"] = op1
        rd = [in0] + [a for a in (s1, s2) if a is not None and not isinstance(a, (int, float))]
        return self.op(eng, lambda e: e.tensor_scalar(out, in0, s1, s2, op0, **kw), rd, [out])

    def stt(self, out, in0, scalar, in1, op0, op1, eng="dve"):
        rd = [in0, in1] + ([scalar] if not isinstance(scalar, (int, float)) else [])
        return self.op(eng, lambda e: e.scalar_tensor_tensor(out, in0, scalar, in1, op0, op1), rd, [out])

    def cp(self, out, in_, eng="dve"):
        if eng == "act":
            return self.op("act", lambda e: e.copy(out, in_), [in_], [out])
        return self.op(eng, lambda e: e.tensor_copy(out, in_), [in_], [out])

    def memset(self, out, val, eng="dve"):
        return self.op(eng, lambda e: e.memset(out, val), [], [out])

    def recip(self, out, in_):
        return self.op("dve", lambda e: e.reciprocal(out, in_), [in_], [out])

    def sqrt(self, out, in_):
        return self.op("act", lambda e: e.sqrt(out, in_), [in_], [out])

    def flush(self, final=False):
        nc = self.nc
        ops = self.ops
        lo, n = self.emitted, len(self.ops)
        if n == lo:
            return 0
        needs = [False] * (n - lo)
        for i in range(lo, n):
            eng, fn, deps, is_dma = ops[i]
            for d in deps:
                de, _, _, ddma = ops[d]
                if ddma or d < lo:
                    continue
                if de == eng and eng == "pe" and not is_dma:
                    continue
                needs[d - lo] = True
        streams = {e: [] for e in self.eng}
        for i in range(lo, n):
            streams[ops[i][0]].append(i)
        for e in self.COMPUTE:
            if streams[e]:
                needs[streams[e][-1] - lo] = True
        self.tok.extend([None] * (n - lo))
        self.extra.extend([None] * (n - lo))
        tok, extra = self.tok, self.extra
        for i in range(lo, n):
            eng, fn, deps, is_dma = ops[i]
            if is_dma:
                j = self.dcnt[eng] % self.NDMA
                self.dcnt[eng] += 1
                self.duse[eng][j] += 1
                tok[i] = (self.dsem[eng][j], 16 * self.duse[eng][j])
                extra[i] = self.dprev[eng][j]
                self.dprev[eng][j] = i
            elif needs[i - lo]:
                self.cnt[eng] += 1
                tok[i] = (self.sem[eng], self.cnt[eng])
        bar = [(self.sem[e], self.cnt[e]) for e in self.COMPUTE if self.cnt[e] > 0]
        for q in self.dsem:
            if q == "pool" and not final:
                continue
            for j in range(self.NDMA):
                if self.duse[q][j] > 0:
                    bar.append((self.dsem[q][j], 16 * self.duse[q][j]))

        def run(ename):
            def body(e):
                seen = self.seen[ename]
                for i in streams[ename]:
                    eng, fn, deps, is_dma = ops[i]
                    want = {}
                    dl = list(deps)
                    if extra[i] is not None:
                        dl.append(extra[i])
                    for d in dl:
                        if tok[d] is None:
                            continue
                        s, v = tok[d]
                        if seen.get(s.num, 0) >= v:
                            continue
                        if want.get(s.num, (None, 0))[1] < v:
                            want[s.num] = (s, v)
                    for s, v in want.values():
                        e.wait_ge(s, v)
                        seen[s.num] = v
                    ins = fn(e)
                    if tok[i] is not None:
                        ins.then_inc(tok[i][0], 16 if is_dma else 1)
                for s, v in bar:
                    if ename in self.COMPUTE and s.num == self.sem[ename].num:
                        continue
                    if seen.get(s.num, 0) < v:
                        e.wait_ge(s, v)
                        seen[s.num] = v
            return body

        with nc.Block() as block:
            block.sync(run("sp"))
            block.tensor(run("pe"))
            block.scalar(run("act"))
            block.vector(run("dve"))
            block.gpsimd(run("pool"))
        self.emitted = n
        self.buf.clear()
        keep = {}
        for k, b in self.dbuf.items():
            w = [i for i in b[0] if i in self.bg_ops]
            if w:
                keep[k] = [w, []]
        self.dbuf = keep
        for i in range(lo, n):
            ops[i][1] = None
            ops[i][2] = None
        return n - lo


class Prog:
    def __init__(self, debug=(), layers=(0, 1), stop_after=None, feed=(), only=None):
        self.debug = set(debug)
        self.feed = set(feed)
        self.only = only
        self.layers = layers
        self.stop_after = stop_after
        self.nc = bass.Bass("TRN2", target_bir_lowering=False)
        self.stack = ExitStack()
        self.s = Sched(self.nc, self.stack)
        self.outs = []

    def din(self, name, shape, dtype=F32):
        return self.s.dram(name, shape, dtype, kind="ExternalInput")

    def scratch(self, name, shape, dtype):
        if name in self.feed:
            return self.s.dram(name, shape, dtype, kind="ExternalInput")
        if name in self.debug:
            self.outs.append(name)
            return self.s.dram(name, shape, dtype, kind="ExternalOutput")
        return self.s.dram(name, shape, dtype, kind="Internal")

    def declare(self):
        L = DEPTH
        self.xin = self.din("xin", [T, D])
        self.cT = self.din("cT", [128, KC, 2])
        self.w_ada = self.din("w_ada", [L, D, 6 * D])
        self.b_ada_col = self.din("b_ada_col", [128, L, 96])
        self.b_ada = self.din("b_ada", [L, 6 * D])
        self.w_in_r = self.din("w_in_r", [L, D, N_IN])
        self.b_gate_col = self.din("b_gate_col", [128, L, 64])
        self.ident_in = self.din("ident", [128, 128])
        self.rope_cos = self.din("rope_cos", [128, NLAT])
        self.rope_sin = self.din("rope_sin", [128, NLAT])
        self.WIN = [self.scratch(f"WIN{l}", [D, N_IN], BF16) for l in range(L)]
        self.XS = self.scratch("XS", [T, D], F32)
        self.MODR = self.scratch("MODR", [L, 2, 2, D], F32)
        self.XBCT = self.scratch("XBCT", [1024, T], F32)
        self.NQT = self.scratch("NQT", [512, T], BF16)
        self.NKT = self.scratch("NKT", [512, T], BF16)
        self.SQT = self.scratch("SQT", [512, T], BF16)
        self.SKT = self.scratch("SKT", [128, T], BF16)
        self.FNT = self.scratch("FNT", [512, T], BF16)
        self.GT = self.scratch("GT", [8192, T], BF16)
        self.ZS = self.scratch("ZS", [T, 512], F32)
        self.NV = self.scratch("NV", [T, 512], BF16)
        self.SV = self.scratch("SV", [T, 128], BF16)
        self.DTR = self.scratch("DTR", [T, 16], F32)
        self.YT = [self.scratch(f"YT{i}", [512, T], BF16) for i in range(4)]
        self.XTM = self.scratch("XTM", [T, 512], F32)
        self.ACCT = self.scratch("ACCT", [D, T], BF16)
        self.X1 = self.scratch("X1", [T, D], F32)
        self.H2T = self.scratch("H2T", [D, T], BF16)
        self.w_branch = self.din("w_branch", [L, 4 * 512, D])
        self.w_out = self.din("w_out", [L, D, D])
        self.ffn_w_up = self.din("ffn_w_up", [L, D, 2 * D_FF])
        self.ffn_w_down = self.din("ffn_w_down", [L, D_FF, D])
        self.WBR = [self.scratch(f"WBR{l}", [4 * 512, D], BF16) for l in range(L)]
        self.WOUT = [self.scratch(f"WOUT{l}", [D, D], BF16) for l in range(L)]
        self.WUP = [self.scratch(f"WUP{l}", [D, 2 * D_FF], BF16) for l in range(L)]
        self.WDN = [self.scratch(f"WDN{l}", [D_FF, D], BF16) for l in range(L)]
        self.ln1_g = self.din("ln1_g", [L, D])
        self.ln1_b = self.din("ln1_b", [L, D])
        self.ln2_g = self.din("ln2_g", [L, D])
        self.ln2_b = self.din("ln2_b", [L, D])
        self.ffn_convw_col = self.din("ffn_convw_col", [128, L, FC, 3])
        self.ffn_convb_col = self.din("ffn_convb_col", [128, L, FC])
        self.out = self.s.dram("out", [NLAT, D], F32, kind="ExternalOutput")
        self.XTMB = self.scratch("XTMB", [T, 512], BF16)
        self.BTM = self.scratch("BTM", [T, 256], BF16)
        self.BCT = self.scratch("BCT", [512, T], BF16)
        self.YF = self.scratch("YF", [T, 512], F32)
        self.ssd_convw_col = self.din("ssd_convw_col", [128, L, 8, 5])
        self.ssd_convb_col = self.din("ssd_convb_col", [128, L, 8])
        self.ssd_dt_bias = self.din("ssd_dt_bias", [L, 16])
        self.ssd_a_log = self.din("ssd_a_log", [L, 16])
        self.ssd_d = self.din("ssd_d", [L, 8])
        self.ssd_norm_g = self.din("ssd_norm_g", [L, 512])
        self.tri_f = self.din("tri_f", [128, 128])
        self.tri_b = self.din("tri_b", [128, 128])
        self.cs128 = self.din("cs128", [128, 256], BF16)
        self.dftc = self.din("dftc", [NLAT, NLAT], BF16)
        self.dfts = self.din("dfts", [NLAT, NLAT], BF16)
        self.c256 = self.din("c256", [NCTX, NCTX], BF16)
        self.s256 = self.din("s256", [NCTX, NCTX], BF16)
        self.band_masks = self.din("band_masks", [128, 2, 128], BF16)
        self.swa_sink = self.din("swa_sink", [L, 8])
        self.na_masks = self.din("na_masks", [128, NA_NB, 128], BF16)
        self.na_bias = self.din("na_bias", [L, 4, 128, 2, 7, 128])

    def phase0(self, light=False):
        s = self.s
        L = DEPTH
        self.ident = s.sb([128, 128], BF16, "ident")
        self.ones_f = s.sb([128, 128], F32, "ones_f")
        self.modc = s.sb([128, L, 64, 2], F32, "modc")
        self.bgate = s.sb([128, L, 64], F32, "bgate")
        with ExitStack() as ph:
            idf = s.sb([128, 128], F32, stack=ph)
            s.dma(idf[:], self.ident_in)
            s.cp(self.ident[:], idf[:])
            s.memset(self.ones_f[:], 1.0)
            s.dma(self.bgate[:], self.b_gate_col)
            if light:
                s.flush()
                return
            for l in self.layers:
                for r in range(0, D, 128):
                    s.dma(self.WIN[l][r:r + 128, :], self.w_in_r[l, r:r + 128, :], queue="pool", bg=True)
                for r in range(0, D, 128):
                    s.dma(self.WBR[l][r:r + 128, :], self.w_branch[l, r:r + 128, :], queue="pool", bg=True)
                    s.dma(self.WOUT[l][r:r + 128, :], self.w_out[l, r:r + 128, :], queue="pool", bg=True)
                    s.dma(self.WUP[l][r:r + 128, :], self.ffn_w_up[l, r:r + 128, :], queue="pool", bg=True)
                for r in range(0, D_FF, 128):
                    s.dma(self.WDN[l][r:r + 128, :], self.ffn_w_down[l, r:r + 128, :], queue="pool", bg=True)
            cT = s.sb([128, KC, 2], F32, stack=ph)
            sT = s.sb([128, KC, 2], F32, stack=ph)
            srep = [s.sb([128, KC, 128], F32, stack=ph) for _ in range(2)]
            bcol = s.sb([128, L, 96], F32, stack=ph)
            brow = s.sb([128, 512], F32, stack=ph)
            wt = [s.sb([128, KC, 512], F32, stack=ph) for _ in range(2)]
            orow = s.sb([128, 512], F32, stack=ph)
            pcol = s.ps([128, 128], F32, stack=ph)
            prow = [s.ps([128, 512], F32, stack=ph) for _ in range(2)]
            s.dma(cT[:], self.cT)
            s.dma(bcol[:], self.b_ada_col)
            s.act(sT[:], cT[:], AF.Silu)
            for m in range(2):
                s.tt(srep[m][:], self.ones_f[:].unsqueeze(1).to_broadcast([128, KC, 128]),
                     sT[:, :, m:m + 1].to_broadcast([128, KC, 128]), ALU.mult)
            ng = 0
            for l in self.layers:
                for g in range(24):
                    c0 = g * 512
                    w = wt[ng % 2]
                    ng += 1
                    s.dma(w[:], self.w_ada[l, :, c0:c0 + 512].rearrange("(k p) n -> p k n", p=128))
                    sec = c0 // D
                    if sec in (2, 5):
                        s.dma(brow[:], self.b_ada[l, c0:c0 + 512].partition_broadcast(128))
                        for m in range(2):
                            p = prow[m]
                            for k in range(KC):
                                s.mm(p[:], srep[m][:, k, :], w[:, k, :], start=(k == 0), stop=(k == KC - 1))
                            s.tt(orow[:], p[:], brow[:], ALU.add)
                            gi = 0 if sec == 2 else 1
                            cc = c0 - sec * D
                            s.dma(self.MODR[l, gi, m, cc:cc + 512].unsqueeze(0), orow[0:1, :])
                    else:
                        cb = {0: 0, 1: 16, 3: 32, 4: 48}[sec] + (c0 % D) // 128
                        for j in range(4):
                            for k in range(KC):
                                s.mm(pcol[:, (cb + j) * 2:(cb + j) * 2 + 2], w[:, k, j * 128:(j + 1) * 128], sT[:, k, :],
                                     start=(k == 0), stop=(k == KC - 1))
                pv = pcol[:].rearrange("p (b m) -> p b m", m=2)
                for (o, srcb) in ((0, 0), (16, 16), (32, 48), (48, 64)):
                    s.tt(self.modc[:, l, o:o + 16, :], pv[:, o:o + 16, :],
                         bcol[:, l, srcb:srcb + 16].unsqueeze(2).to_broadcast([128, 16, 2]), ALU.add)
                for o in (16, 48):
                    s.ts(self.modc[:, l, o:o + 16, :], self.modc[:, l, o:o + 16, :], 1.0, None, op0=ALU.add)
            s.flush()

    def ln_rows(self, xt, yb, junk, st):
        s = self.s
        s.memset(st[:], 0.0)
        s.act(junk[:], xt[:], AF.Identity, accum_out=st[:, 0:1])
        s.act(junk[:], xt[:], AF.Square, accum_out=st[:, 1:2])
        s.ts(st[:, 2:3], st[:, 0:1], 1.0 / D, None, op0=ALU.mult)
        s.ts(st[:, 3:4], st[:, 1:2], 1.0 / D, None, op0=ALU.mult)
        s.tt(st[:, 4:5], st[:, 2:3], st[:, 2:3], ALU.mult)
        s.tt(st[:, 5:6], st[:, 3:4], st[:, 4:5], ALU.subtract)
        s.ts(st[:, 6:7], st[:, 5:6], LN_EPS, None, op0=ALU.add)
        s.sqrt(st[:, 6:7], st[:, 6:7])
        s.recip(st[:, 6:7], st[:, 6:7])
        s.stt(st[:, 7:8], st[:, 2:3], -1.0, st[:, 6:7], ALU.mult, ALU.mult)
        s.act(yb[:], xt[:], AF.Identity, bias=st[:, 7:8], scale=st[:, 6:7])

    def mod_transpose(self, yb, hT, col0, tp, tmp, sc, sh):
        s = self.s
        for half in range(2):
            p = tp[half]
            for kk in range(8):
                k = half * 8 + kk
                s.tr(p[:, kk * 128:(kk + 1) * 128], yb[:, k * 128:(k + 1) * 128], self.ident[:])
            pv = p[:].rearrange("p (k t) -> p k t", t=128)
            s.tt(tmp[:], pv, sc[:, half * 8:half * 8 + 8].unsqueeze(2).to_broadcast([128, 8, 128]), ALU.mult)
            s.tt(hT[:, half * 8:half * 8 + 8, col0:col0 + 128], tmp[:],
                 sh[:, half * 8:half * 8 + 8].unsqueeze(2).to_broadcast([128, 8, 128]), ALU.add)

    def phase_inproj(self, l, src):
        s = self.s
        with ExitStack() as ph:
            xt = [s.sb([128, D], F32, stack=ph) for _ in range(2)]
            yb = [s.sb([128, D], BF16, stack=ph) for _ in range(2)]
            junk = s.sb([128, D], BF16, stack=ph)
            st = [s.sb([128, 8], F32, stack=ph) for _ in range(2)]
            tmp = s.sb([128, 8, 128], F32, stack=ph)
            hT = [s.sb([128, KC, 512], BF16, stack=ph) for _ in range(2)]
            NWT = 4
            wt = [s.sb([128, KC, 512], BF16, stack=ph) for _ in range(NWT)]
            sg_f = [s.sb([128, 4, 512], F32, stack=ph) for _ in range(2)]
            sg_b = [s.sb([128, 4, 512], BF16, stack=ph) for _ in range(2)]
            cos_t = s.sb([128, 512], F32, stack=ph)
            sin_t = s.sb([128, 512], F32, stack=ph)
            r1 = s.sb([128, 512], F32, stack=ph)
            r2 = s.sb([128, 512], F32, stack=ph)
            tp = [s.ps([128, 1024], BF16, stack=ph) for _ in range(2)]
            pm = [s.ps([128, 512], F32, stack=ph) for _ in range(4)]
            WIN = self.WIN[l]
            nw = 0
            nsg = 0
            npm = 0
            nln = [0]

            def ln_sub(tj, a):
                tt0, _ = TILES[tj]
                mm_ = 1 if tj == 0 else 0
                x_, y_, st_ = xt[nln[0] % 2], yb[nln[0] % 2], st[nln[0] % 2]
                nln[0] += 1
                s.dma(x_[:], src[tt0 + a * 128:tt0 + (a + 1) * 128, :])
                self.ln_rows(x_, y_, junk, st_)
                self.mod_transpose(y_, hT[tj % 2], a * 128, tp, tmp, self.modc[:, l, 16:32, mm_], self.modc[:, l, 0:16, mm_])

            for a in range(TILES[0][1] // 128):
                ln_sub(0, a)
            LN_AT = {3: 0, 9: 1, 15: 2, 21: 3}
            for ti, (t0, nt) in enumerate(TILES):
                m = 1 if ti == 0 else 0
                is_ctx = ti == 0
                h = hT[ti % 2]
                if not is_ctx:
                    l0 = t0 - NCTX
                    s.dma(cos_t[:, :nt], self.rope_cos[:, l0:l0 + nt])
                    s.dma(sin_t[:, :nt], self.rope_sin[:, l0:l0 + nt])
                for gi, (kind, gidx) in enumerate(FM_GROUPS):
                    c0 = gi * 512
                    nb = 4 if kind != "ropek" else 2
                    if gi in LN_AT and ti + 1 < len(TILES):
                        ln_sub(ti + 1, LN_AT[gi])
                    w = wt[nw % NWT]
                    nw += 1
                    s.dma(w[:, :, :nb * 128], WIN[:, c0:c0 + nb * 128].rearrange("(k p) n -> p k n", p=128))
                    if kind in ("xbc",):
                        sg = sg_f[nsg % 2]
                    else:
                        sg = sg_b[nsg % 2]
                    nsg += 1
                    ps_blocks = []
                    for j in range(nb):
                        p = pm[npm % 4]
                        npm += 1
                        for k in range(KC):
                            s.mm(p[:, :nt], w[:, k, j * 128:(j + 1) * 128], h[:, k, :nt], start=(k == 0), stop=(k == KC - 1))
                        if kind == "xbc":
                            s.cp(sg[:, j, :nt], p[:, :nt], eng="act")
                        elif kind in ("nq", "nk", "fn"):
                            s.cp(sg[:, j, :nt], p[:, :nt], eng="act" if j % 2 == 0 else "dve")
                        elif kind == "gate":
                            blk = gidx * 4 + j
                            s.act(sg[:, j, :nt], p[:, :nt], AF.Sigmoid, bias=self.bgate[:, l, blk:blk + 1])
                        elif kind in ("rope", "ropek"):
                            if j % 2 == 0:
                                if is_ctx:
                                    s.cp(sg[:, j // 2, :nt], p[:, :nt], eng="act")
                                else:
                                    s.tt(r1[:, :nt], p[:, :nt], cos_t[:, :nt], ALU.mult)
                            else:
                                if not is_ctx:
                                    s.tt(r2[:, :nt], p[:, :nt], sin_t[:, :nt], ALU.mult)
                                    s.tt(sg[:, j // 2, :nt], r1[:, :nt], r2[:, :nt], ALU.add)
                    if kind == "xbc":
                        dst = self.XBCT[gidx * 512:(gidx + 1) * 512, t0:t0 + nt]
                        nbo = 4
                    elif kind == "nq":
                        dst, nbo = self.NQT[:, t0:t0 + nt], 4
                    elif kind == "nk":
                        dst, nbo = self.NKT[:, t0:t0 + nt], 4
                    elif kind == "fn":
                        dst, nbo = self.FNT[:, t0:t0 + nt], 4
                    elif kind == "gate":
                        dst, nbo = self.GT[gidx * 512:(gidx + 1) * 512, t0:t0 + nt], 4
                    elif kind == "rope":
                        dst, nbo = self.SQT[gidx * 256:(gidx + 1) * 256, t0:t0 + nt], 2
                    else:
                        dst, nbo = self.SKT[:, t0:t0 + nt], 1
                    s.dma(dst.rearrange("(j p) t -> p j t", p=128), sg[:, :nbo, :nt], queue=STQ)
                for (cs, ncol, kind) in ((N_FM, 512, "z"), (N_FM + 512, 512, "nv"), (N_FM + 1024, 144, "svdt")):
                    w = wt[nw % NWT]
                    nw += 1
                    s.dma(w[:, :, :ncol], WIN[:, cs:cs + ncol].rearrange("(k p) n -> p k n", p=128))
                    sg = sg_f[nsg % 2] if kind != "nv" else sg_b[nsg % 2]
                    sgb = sg_b[nsg % 2]
                    nsg += 1
                    for a in range(nt // 128):
                        p = pm[npm % 4]
                        npm += 1
                        for k in range(KC):
                            s.mm(p[:, :ncol], h[:, k, a * 128:(a + 1) * 128], w[:, k, :ncol], start=(k == 0), stop=(k == KC - 1))
                        if kind == "z":
                            s.act(sg[:, a, :], p[:], AF.Silu)
                        elif kind == "nv":
                            s.cp(sg[:, a, :], p[:], eng="dve")
                        else:
                            s.cp(sgb[:, a, 0:128], p[:, 0:128], eng="dve")
                            s.cp(sg[:, a, 0:16], p[:, 128:144], eng="dve")
                    na = nt // 128
                    if kind == "z":
                        s.dma(self.ZS[t0:t0 + nt, :].rearrange("(a p) n -> p a n", p=128), sg[:, :na, :], queue=STQ)
                    elif kind == "nv":
                        s.dma(self.NV[t0:t0 + nt, :].rearrange("(a p) n -> p a n", p=128), sg[:, :na, :], queue=STQ)
                    else:
                        s.dma(self.SV[t0:t0 + nt, :].rearrange("(a p) n -> p a n", p=128), sgb[:, :na, 0:128], queue=STQ)
                        s.dma(self.DTR[t0:t0 + nt, :].rearrange("(a p) n -> p a n", p=128), sg[:, :na, 0:16], queue=STQ)
            s.flush()

    def phase_fnet(self, l, ctx_out):
        s = self.s
        with ExitStack() as ph:
            U = s.sb([128, 34, 4, 256], BF16, stack=ph)
            with ExitStack() as ph1:
                fT = s.sb([128, 4, T], BF16, stack=ph1)
                cs = s.sb([128, 256], BF16, stack=ph1)
                pu = [s.ps([128, 512], F32, stack=ph1) for _ in range(4)]
                s.dma(cs[:], self.cs128)
                for g in range(4):
                    s.dma(fT[:, g, :], self.FNT[g * 128:(g + 1) * 128, :])
                n = 0
                for a in range(34):
                    for gp in range(2):
                        p = pu[n % 4]
                        n += 1
                        for gg in range(2):
                            s.mm(p[:, gg * 256:(gg + 1) * 256], fT[:, gp * 2 + gg, a * 128:(a + 1) * 128], cs[:])
                        s.cp(U[:, a, gp * 2:gp * 2 + 2, :], p[:].rearrange("p (g c) -> p g c", g=2), eng="act" if n % 2 else "dve")
                s.flush()
            with ExitStack() as ph2:
                cbuf = [s.sb([128, 8, 512], BF16, stack=ph2) for _ in range(4)]
                sbuf = [s.sb([128, 8, 512], BF16, stack=ph2) for _ in range(4)]
                ystage = [s.sb([128, 4, 512], BF16, stack=ph2) for _ in range(2)]
                c256 = s.sb([128, 2, 256], BF16, stack=ph2)
                s256 = s.sb([128, 2, 256], BF16, stack=ph2)
                pacc = [s.ps([128, 512], F32, stack=ph2) for _ in range(4)]
                YT = self.YT[3]
                if ctx_out:
                    s.dma(c256[:], self.c256.rearrange("(a p) k -> p a k", p=128))
                    s.dma(s256[:], self.s256.rearrange("(a p) k -> p a k", p=128))
                    ys = ystage[1]
                    for g in range(4):
                        for a in range(2):
                            s.mm(pacc[g][:, :256], U[:, a, g, 0:128], c256[:, a, :], start=(a == 0), stop=False)
                            s.mm(pacc[g][:, :256], U[:, a, g, 128:256], s256[:, a, :], start=False, stop=(a == 1))
                        s.cp(ys[:, g, :256], pacc[g][:, :256], eng="act" if g % 2 else "dve")
                    s.dma(YT[:, 0:256].rearrange("(g p) t -> p g t", p=128), ys[:, :, :256], queue=STQ)
                nb = 0
                for kb in range(8):
                    ys = ystage[kb % 2]
                    for ncn in range(4):
                        cb_, sb_ = cbuf[nb % 4], sbuf[nb % 4]
                        nb += 1
                        s.dma(cb_[:], self.dftc[ncn * 1024:(ncn + 1) * 1024, kb * 512:(kb + 1) * 512].rearrange("(a p) k -> p a k", p=128))
                        s.dma(sb_[:], self.dfts[ncn * 1024:(ncn + 1) * 1024, kb * 512:(kb + 1) * 512].rearrange("(a p) k -> p a k", p=128))
                        for g in range(4):
                            for a in range(8):
                                nt = ncn * 8 + a
                                s.mm(pacc[g][:], U[:, 2 + nt, g, 0:128], cb_[:, a, :], start=(nt == 0), stop=False)
                                s.mm(pacc[g][:], U[:, 2 + nt, g, 128:256], sb_[:, a, :], start=False, stop=(nt == 31))
                    for g in range(4):
                        s.cp(ys[:, g, :], pacc[g][:], eng="act" if g % 2 else "dve")
                    s.dma(YT[:, NCTX + kb * 512:NCTX + (kb + 1) * 512].rearrange("(g p) t -> p g t", p=128), ys[:], queue=STQ)
                s.flush()

    def phase_swa(self, l, ctx_out):
        s = self.s
        with ExitStack() as ph:
            QT = s.sb([64, 8, T], BF16, stack=ph)
            KT = s.sb([64, 2, T], BF16, stack=ph)
            Vraw = s.sb([128, 34, 128], BF16, stack=ph)
            Vaug = s.sb([128, 34, 2, 65], BF16, stack=ph)
            masks = s.sb([128, 2, 128], BF16, stack=ph)
            sinkb = s.sb([128, 8], F32, stack=ph)
            E = [s.sb([128, 5, 4, 128], BF16, stack=ph) for _ in range(2)]
            yb = [s.sb([128, 4, 64], BF16, stack=ph) for _ in range(2)]
            den = [s.sb([128, 4], F32, stack=ph) for _ in range(2)]
            stage = [s.sb([128, 2, 512], BF16, stack=ph) for _ in range(2)]
            pS = [s.ps([128, 512], F32, stack=ph) for _ in range(3)]
            pO = [s.ps([128, 512], F32, stack=ph) for _ in range(2)]
            pT = [s.ps([128, 1024], BF16, stack=ph) for _ in range(2)]
            for h in range(8):
                s.dma(QT[:, h, :], self.SQT[h * 64:(h + 1) * 64, :])
            for hk in range(2):
                s.dma(KT[:, hk, :], self.SKT[hk * 64:(hk + 1) * 64, :])
            s.dma(Vraw[:], self.SV.rearrange("(a p) c -> p a c", p=128))
            s.memset(Vaug[:], 1.0)
            s.cp(Vaug[:, :, :, 0:64], Vraw[:].rearrange("p a (h d) -> p a h d", h=2))
            s.dma(masks[:], self.band_masks)
            s.dma(sinkb[:], self.swa_sink[l].partition_broadcast(128))
            s.act(sinkb[:], sinkb[:], AF.Exp)
            YT = self.YT[2]
            qtiles = ([0, 1] if ctx_out else []) + list(range(2, 34))
            cnt = {"nps": 0, "nst": 0}
            iters = []
            for hk in range(2):
                for qi, qt in enumerate(qtiles):
                    keys = [(0, None), (1, None)]
                    if qt >= 2:
                        if qt - 1 >= 2:
                            keys.append((qt - 1, 0))
                        keys.append((qt, None))
                        if qt + 1 <= 33:
                            keys.append((qt + 1, 1))
                    idx = len(iters)
                    e, po, y_, dn, pt = E[idx % 2], pO[idx % 2], yb[idx % 2], den[idx % 2], pT[idx % 2]

                    def st1(hk=hk, qt=qt, keys=keys, e=e):
                        for ki, (kt, mk) in enumerate(keys):
                            p = pS[cnt["nps"] % 3]
                            cnt["nps"] += 1
                            for g in range(4):
                                s.mm(p[:, g * 128:(g + 1) * 128], KT[:, hk, kt * 128:(kt + 1) * 128],
                                     QT[:, 4 * hk + g, qt * 128:(qt + 1) * 128])
                            s.act(e[:, ki, :, :], p[:].rearrange("p (g q) -> p g q", g=4), AF.Exp, scale=0.125)
                            if mk is not None:
                                s.tt(e[:, ki, :, :], e[:, ki, :, :], masks[:, mk, :].unsqueeze(1).to_broadcast([128, 4, 128]), ALU.mult)

                    def st2(hk=hk, qt=qt, keys=keys, e=e, po=po, y_=y_, dn=dn, pt=pt):
                        for g in range(4):
                            for ki, (kt, mk) in enumerate(keys):
                                s.mm(po[:, g * 65:(g + 1) * 65], e[:, ki, g, :], Vaug[:, kt, hk, :], start=(ki == 0), stop=(ki == len(keys) - 1))
                        pov = po[:, 0:260].rearrange("p (g c) -> p g c", c=65)
                        s.tt(dn[:], pov[:, :, 64], sinkb[:, 4 * hk:4 * hk + 4], ALU.add)
                        s.recip(dn[:], dn[:])
                        s.tt(y_[:], pov[:, :, 0:64], dn[:].unsqueeze(2).to_broadcast([128, 4, 64]), ALU.mult)
                        yv = y_[:].rearrange("p g d -> p (g d)")
                        for j in range(2):
                            s.tr(pt[:, j * 128:(j + 1) * 128], yv[:, j * 128:(j + 1) * 128], self.ident[:])
                        if qt < 2:
                            slot, t0g, width, last = qt, 0, 256, qt == 1
                        else:
                            li = qt - 2
                            slot, t0g, width, last = li % 4, NCTX + (li // 4) * 512, 512, li % 4 == 3
                        stg = stage[cnt["nst"] % 2]
                        s.cp(stg[:, :, slot * 128:(slot + 1) * 128], pt[:, 0:256].rearrange("p (j t) -> p j t", j=2), eng="act")
                        if last:
                            s.dma(YT[hk * 256:(hk + 1) * 256, t0g:t0g + width].rearrange("(j p) t -> p j t", p=128), stg[:, :, :width], queue=STQ)
                            cnt["nst"] += 1
                    iters.append((st1, st2))
            iters[0][0]()
            for i in range(len(iters)):
                if i + 1 < len(iters):
                    iters[i + 1][0]()
                iters[i][1]()
            s.flush()

    def phase_na(self, l, ctx_out):
        s = self.s
        NB = NA_NB
        with ExitStack() as ph:
            masks = s.sb([128, NB, 128], BF16, stack=ph)
            ebm = s.sb([128, 2, NB, 128], BF16, stack=ph)
            bg = s.sb([128, 2, 7, 128], F32, stack=ph)
            eb = s.sb([128, 2, 7, 128], BF16, stack=ph)
            QT = s.sb([128, T], BF16, stack=ph)
            KT = s.sb([128, T], BF16, stack=ph)
            Vraw = s.sb([128, 34, 128], BF16, stack=ph)
            Vaug = s.sb([128, 34, 2, 65], BF16, stack=ph)
            E = [s.sb([128, 7, 128], BF16, stack=ph) for _ in range(2)]
            yb = [s.sb([128, 2, 64], BF16, stack=ph) for _ in range(2)]
            den = [s.sb([128, 2], F32, stack=ph) for _ in range(2)]
            stage = [s.sb([128, 512], BF16, stack=ph) for _ in range(2)]
            pS = [s.ps([128, 512], F32, stack=ph) for _ in range(4)]
            pO = [s.ps([128, 512], F32, stack=ph) for _ in range(2)]
            pT = [s.ps([128, 1024], BF16, stack=ph) for _ in range(2)]
            s.dma(masks[:], self.na_masks)
            YT = self.YT[1]
            qtiles = ([0, 1] if ctx_out else []) + list(range(2, 34))
            n = 0
            cnt = {"nps": 0, "nst": 0}
            for blk in range(4):
                s.dma(QT[:], self.NQT[blk * 128:(blk + 1) * 128, :])
                s.dma(KT[:], self.NKT[blk * 128:(blk + 1) * 128, :])
                s.dma(Vraw[:], self.NV[:, blk * 128:(blk + 1) * 128].rearrange("(a p) c -> p a c", p=128))
                s.memset(Vaug[:], 1.0)
                s.cp(Vaug[:, :, :, 0:64], Vraw[:].rearrange("p a (h d) -> p a h d", h=2))
                s.dma(bg[:], self.na_bias[l, blk])
                s.act(eb[:], bg[:], AF.Exp)
                for hh in range(2):
                    for bi in range(NB):
                        s.tt(ebm[:, hh, bi, :], eb[:, hh, NA_BLOCK_DI[bi], :], masks[:, bi, :], ALU.mult)
                iters = []
                for qi, qt in enumerate(qtiles):
                    po, y_, dn, pt = pO[n % 2], yb[n % 2], den[n % 2], pT[n % 2]
                    n += 1
                    for hh in range(2):
                        if qt < 2:
                            keys, off, nlat = [0, 1], 0, 0
                        else:
                            off, kts = NA_CFG[qt - 2]
                            nlat = len(kts)
                            keys = [2 + k for k in kts] + [0, 1]
                        e = E[len(iters) % 2]

                        def st1(qt=qt, hh=hh, keys=keys, off=off, nlat=nlat, e=e):
                            base = 64 * hh
                            p0, p1 = pS[(2 * cnt["nps"]) % 4], pS[(2 * cnt["nps"] + 1) % 4]
                            cnt["nps"] += 1
                            nk = len(keys)
                            for ki, kt in enumerate(keys):
                                p = p0 if ki < 4 else p1
                                s.mm(p[:, (ki % 4) * 128:(ki % 4 + 1) * 128], KT[base:base + 64, kt * 128:(kt + 1) * 128],
                                     QT[base:base + 64, qt * 128:(qt + 1) * 128])
                            n0 = min(nk, 4)
                            s.act(e[:, 0:n0, :], p0[:, :n0 * 128].rearrange("p (k q) -> p k q", q=128), AF.Exp, scale=0.125)
                            if nk > 4:
                                s.act(e[:, 4:nk, :], p1[:, :(nk - 4) * 128].rearrange("p (k q) -> p k q", q=128), AF.Exp, scale=0.125)
                            if nlat:
                                s.tt(e[:, 0:nlat, :], e[:, 0:nlat, :], ebm[:, hh, off:off + nlat, :], ALU.mult)

                        def st2(qt=qt, hh=hh, keys=keys, e=e, po=po, y_=y_, dn=dn, pt=pt):
                            nk = len(keys)
                            for ki, kt in enumerate(keys):
                                s.mm(po[:, hh * 65:(hh + 1) * 65], e[:, ki, :], Vaug[:, kt, hh, :], start=(ki == 0), stop=(ki == nk - 1))
                            if hh == 0:
                                return
                            pov = po[:, 0:130].rearrange("p (g c) -> p g c", c=65)
                            s.cp(dn[:], pov[:, :, 64])
                            s.recip(dn[:], dn[:])
                            s.tt(y_[:], pov[:, :, 0:64], dn[:].unsqueeze(2).to_broadcast([128, 2, 64]), ALU.mult)
                            s.tr(pt[:, 0:128], y_[:].rearrange("p g d -> p (g d)"), self.ident[:])
                            if qt < 2:
                                slot, t0g, width, last = qt, 0, 256, qt == 1
                            else:
                                li = qt - 2
                                slot, t0g, width, last = li % 4, NCTX + (li // 4) * 512, 512, li % 4 == 3
                            stg = stage[cnt["nst"] % 2]
                            s.cp(stg[:, slot * 128:(slot + 1) * 128], pt[:, 0:128], eng="act")
                            if last:
                                s.dma(YT[blk * 128:(blk + 1) * 128, t0g:t0g + width], stg[:, :width], queue=STQ)
                                cnt["nst"] += 1
                        iters.append((st1, st2))
                iters[0][0]()
                for i in range(len(iters)):
                    if i + 1 < len(iters):
                        iters[i + 1][0]()
                    iters[i][1]()
            s.flush()

    def phase_ssd(self, l, ctx_out):
        s = self.s
        NP = T + 6
        NO = T + 2

        def i0_of(a):
            return a * 128 if a < 2 else 258 + (a - 2) * 128

        with ExitStack() as ph:
            xp = [s.sb([128, NP], F32, stack=ph) for _ in range(2)]
            acc = s.sb([128, NO], F32, stack=ph)
            u = s.sb([128, NO], F32, stack=ph)
            ub = s.sb([128, NO], BF16, stack=ph)
            stf = s.sb([128, 34, 128], F32, stack=ph)
            stb = s.sb([128, 34, 128], BF16, stack=ph)
            cw = s.sb([128, 8, 5], F32, stack=ph)
            cb = s.sb([128, 8], F32, stack=ph)
            pt = [s.ps([128, 512], F32, stack=ph) for _ in range(4)]
            identf = s.sb([128, 128], F32, stack=ph)
            s.dma(identf[:], self.ident_in)
            s.dma(cw[:], self.ssd_convw_col[:, l])
            s.dma(cb[:], self.ssd_convb_col[:, l])
            for b in range(2):
                s.memset(xp[b][:, 0:2], 0.0)
                s.memset(xp[b][:, 258:260], 0.0)
                s.memset(xp[b][:, NP - 2:NP], 0.0)
            npt = 0
            for blk in range(8):
                x_ = xp[blk % 2]
                s.dma(x_[:, 2:258], self.XBCT[blk * 128:(blk + 1) * 128, 0:256])
                s.dma(x_[:, 260:260 + NLAT], self.XBCT[blk * 128:(blk + 1) * 128, 256:T])
                s.ts(acc[:], x_[:, 0:NO], cw[:, blk, 0:1], None, op0=ALU.mult)
                for k in range(1, 5):
                    s.stt(acc[:], x_[:, k:k + NO], cw[:, blk, k:k + 1], acc[:], ALU.mult, ALU.add)
                s.act(u[:], acc[:], AF.Silu, bias=cb[:, blk:blk + 1])
                if blk >= 4:
                    s.cp(ub[:], u[:], eng="pool")
                    s.dma(self.BCT[(blk - 4) * 128:(blk - 3) * 128, 0:256], ub[:, 0:256], queue=STQ)
                    s.dma(self.BCT[(blk - 4) * 128:(blk - 3) * 128, 256:T], ub[:, 258:258 + NLAT], queue=STQ)
                if blk < 6:
                    for a4 in range(0, 34, 4):
                        p = pt[npt % 4]
                        npt += 1
                        na = min(4, 34 - a4)
                        for j in range(na):
                            i0 = i0_of(a4 + j)
                            s.mm(p[:, j * 128:(j + 1) * 128], u[:, i0:i0 + 128], identf[:])
                        pv = p[:, :na * 128].rearrange("p (a c) -> p a c", c=128)
                        if blk < 4:
                            s.cp(stf[:, a4:a4 + na, :], pv, eng="act")
                        s.cp(stb[:, a4:a4 + na, :], pv, eng="dve")
                    if blk < 4:
                        s.dma(self.XTM[:, blk * 128:(blk + 1) * 128].rearrange("(a p) c -> p a c", p=128), stf[:], queue=STQ)
                        s.dma(self.XTMB[:, blk * 128:(blk + 1) * 128].rearrange("(a p) c -> p a c", p=128), stb[:], queue=STQ)
                    else:
                        s.dma(self.BTM[:, (blk - 4) * 128:(blk - 3) * 128].rearrange("(a p) c -> p a c", p=128), stb[:], queue=STQ)
            s.flush()

        with ExitStack() as ph:
            dt_all = s.sb([128, 34, 16], F32, stack=ph)
            dA_all = s.sb([128, 34, 16], F32, stack=ph)
            bias16 = s.sb([128, 16], F32, stack=ph)
            a16 = s.sb([128, 16], F32, stack=ph)
            dsk = s.sb([128, 8], F32, stack=ph)
            normg = s.sb([128, 512], F32, stack=ph)
            tri = [s.sb([128, 128], F32, stack=ph) for _ in range(2)]
            xb = s.sb([128, 34, 512], BF16, stack=ph)
            Bt = s.sb([128, 34, 256], BF16, stack=ph)
            BCT = s.sb([128, 4, T], BF16, stack=ph)
            H = s.sb([128, 8, 64], F32, stack=ph)
            Hb = s.sb([128, 8, 64], BF16, stack=ph)
            rhsA = s.sb([128, 8, 128], F32, stack=ph)
            MD = s.sb([128, 8, 128], F32, stack=ph)
            S_ = s.sb([128, 8, 128], F32, stack=ph)
            Wt = s.sb([128, 8, 128], BF16, stack=ph)
            acol = s.sb([128, 8], F32, stack=ph)
            ea = s.sb([128, 8], F32, stack=ph)
            de = s.sb([128, 8], F32, stack=ph)
            dec = s.sb([128, 8], F32, stack=ph)
            xw = s.sb([128, 8, 64], BF16, stack=ph)
            tmpy = s.sb([128, 8, 64], F32, stack=ph)
            yo = s.sb([128, 512], F32, stack=ph)
            xf = [s.sb([128, 512], F32, stack=ph) for _ in range(2)]
            yf = [s.sb([128, 512], F32, stack=ph) for _ in range(2)]
            zs = [s.sb([128, 512], F32, stack=ph) for _ in range(2)]
            gg = s.sb([128, 512], F32, stack=ph)
            junk = s.sb([128, 512], F32, stack=ph)
            ss = s.sb([128, 2], F32, stack=ph)
            ob = s.sb([128, 512], BF16, stack=ph)
            stage = [s.sb([128, 4, 512], BF16, stack=ph) for _ in range(2)]
            pAR = [s.ps([128, 512], F32, stack=ph) for _ in range(2)]
            pCB = s.ps([128, 512], F32, stack=ph)
            pY = s.ps([128, 512], F32, stack=ph)
            pYo = s.ps([128, 512], F32, stack=ph)
            pST = s.ps([128, 512], F32, stack=ph)
            pT = s.ps([128, 1024], BF16, stack=ph)
            s.dma(dt_all[:], self.DTR.rearrange("(a p) c -> p a c", p=128))
            s.dma(bias16[:], self.ssd_dt_bias[l].partition_broadcast(128))
            s.dma(a16[:], self.ssd_a_log[l].partition_broadcast(128))
            s.dma(dsk[:], self.ssd_d[l].partition_broadcast(128))
            s.dma(normg[:], self.ssd_norm_g[l].partition_broadcast(128))
            s.dma(tri[0][:], self.tri_f)
            s.dma(tri[1][:], self.tri_b)
            s.dma(xb[:], self.XTMB.rearrange("(a p) c -> p a c", p=128))
            s.dma(Bt[:], self.BTM.rearrange("(a p) c -> p a c", p=128))
            for j in range(4):
                s.dma(BCT[:, j, :], self.BCT[j * 128:(j + 1) * 128, :])
            s.tt(dt_all[:], dt_all[:], bias16[:].unsqueeze(1).to_broadcast([128, 34, 16]), ALU.add)
            s.act(dt_all[:], dt_all[:], AF.Exp)
            s.act(dt_all[:], dt_all[:], AF.Ln, bias=1.0)
            s.act(a16[:], a16[:], AF.Exp)
            s.ts(a16[:], a16[:], -1.0, None, op0=ALU.mult)
            s.tt(dA_all[:], dt_all[:], a16[:].unsqueeze(1).to_broadcast([128, 34, 16]), ALU.mult)
            nld = 0
            for d in range(2):
                order = [0, 1] + list(range(2, 34)) if d == 0 else [1, 0] + list(range(33, 1, -1))
                TR = tri[d]
                last_col = 127 if d == 0 else 0
                s.memset(H[:], 0.0)
                s.memset(Hb[:], 0.0)
                nst = 0
                for a in order:
                    want_y = ctx_out or a >= 2
                    cs_ = slice(a * 128, (a + 1) * 128)
                    dA = dA_all[:, a, d * 8:(d + 1) * 8]
                    dt = dt_all[:, a, d * 8:(d + 1) * 8]
                    s.tt(rhsA[:], TR[:].unsqueeze(1).to_broadcast([128, 8, 128]), dA.unsqueeze(2).to_broadcast([128, 8, 128]), ALU.mult)
                    for g in range(2):
                        s.mm(pAR[g][:], self.ones_f[:], rhsA[:, 4 * g:4 * g + 4, :].rearrange("p h i -> p (h i)"))
                    s.mm(pCB[:, 256:264], TR[:], dA)
                    s.cp(acol[:], pCB[:, 256:264])
                    if want_y:
                        for g in range(2):
                            s.mm(pCB[:, g * 128:(g + 1) * 128], BCT[:, g, cs_], BCT[:, 2 + g, cs_])
                        s.tt(MD[:], TR[:].unsqueeze(1).to_broadcast([128, 8, 128]), dt.unsqueeze(2).to_broadcast([128, 8, 128]), ALU.mult, eng="pool")
                        for g in range(2):
                            hs = slice(4 * g, 4 * g + 4)
                            s.tt(S_[:, hs, :], pAR[g][:].rearrange("p (h i) -> p h i", h=4),
                                 acol[:, hs].unsqueeze(2).to_broadcast([128, 4, 128]), ALU.subtract)
                            s.ts(S_[:, hs, :], S_[:, hs, :], 0.0, None, op0=ALU.min)
                            s.act(S_[:, hs, :], S_[:, hs, :], AF.Exp)
                            s.tt(S_[:, hs, :], S_[:, hs, :], MD[:, hs, :], ALU.mult)
                            s.tt(Wt[:, hs, :], S_[:, hs, :], pCB[:, g * 128:(g + 1) * 128].unsqueeze(1).to_broadcast([128, 4, 128]), ALU.mult)
                        for h in range(8):
                            s.mm(pY[:, h * 64:(h + 1) * 64], Wt[:, h, :], xb[:, a, h * 64:(h + 1) * 64])
                        for g in range(2):
                            s.mm(pYo[:, g * 256:(g + 1) * 256], BCT[:, 2 + g, cs_], Hb[:, 4 * g:4 * g + 4, :].rearrange("p h d -> p (h d)"))
                        s.act(ea[:], acol[:], AF.Exp)
                        s.tt(tmpy[:], pYo[:].rearrange("p (h d) -> p h d", h=8), ea[:].unsqueeze(2).to_broadcast([128, 8, 64]), ALU.mult)
                        s.tt(yo[:], tmpy[:].rearrange("p h d -> p (h d)"), pY[:], ALU.add)
                        if d == 0:
                            x_ = xf[nld % 2]
                            nld += 1
                            s.dma(x_[:], self.XTM[cs_, :])
                            s.tt(x_[:].rearrange("p (h d) -> p h d", h=8), x_[:].rearrange("p (h d) -> p h d", h=8),
                                 dsk[:].unsqueeze(2).to_broadcast([128, 8, 64]), ALU.mult, eng="pool")
                            s.tt(x_[:], x_[:], yo[:], ALU.add)
                            s.dma(self.YF[cs_, :], x_[:], queue=STQ)
                        else:
                            y_ = yf[nld % 2]
                            z_ = zs[nld % 2]
                            nld += 1
                            s.dma(y_[:], self.YF[cs_, :])
                            s.dma(z_[:], self.ZS[cs_, :])
                            s.tt(y_[:], y_[:], yo[:], ALU.add)
                            s.tt(gg[:], y_[:], z_[:], ALU.mult)
                            s.memset(ss[:], 0.0)
                            s.act(junk[:], gg[:], AF.Square, accum_out=ss[:, 0:1])
                            s.ts(ss[:, 1:2], ss[:, 0:1], 1.0 / 512, LN_EPS, op0=ALU.mult, op1=ALU.add)
                            s.sqrt(ss[:, 1:2], ss[:, 1:2])
                            s.recip(ss[:, 1:2], ss[:, 1:2])
                            s.stt(ob[:], gg[:], ss[:, 1:2], normg[:], ALU.mult, ALU.mult)
                            for j in range(4):
                                s.tr(pT[:, j * 128:(j + 1) * 128], ob[:, j * 128:(j + 1) * 128], self.ident[:])
                            if a < 2:
                                slot, t0g, width, lastt = a, 0, 256, a == 0
                            else:
                                li = a - 2
                                slot, t0g, width, lastt = li % 4, NCTX + (li // 4) * 512, 512, li % 4 == 0
                            stg = stage[nst % 2]
                            s.cp(stg[:, :, slot * 128:(slot + 1) * 128], pT[:, 0:512].rearrange("p (j t) -> p j t", j=4), eng="act")
                            if lastt:
                                s.dma(self.YT[0][:, t0g:t0g + width].rearrange("(j p) t -> p j t", p=128), stg[:, :, :width], queue=STQ)
                                nst += 1
                    tot = [pAR[g][:, last_col:512:128] for g in range(2)]
                    for g in range(2):
                        s.tt(de[:, 4 * g:4 * g + 4], tot[g], acol[:, 4 * g:4 * g + 4], ALU.subtract)
                        s.cp(dec[:, 4 * g:4 * g + 4], tot[g])
                    s.act(de[:], de[:], AF.Exp)
                    s.act(dec[:], dec[:], AF.Exp)
                    s.tt(de[:], de[:], dt, ALU.mult)
                    s.tt(xw[:], xb[:, a, :].rearrange("p (h d) -> p h d", h=8), de[:].unsqueeze(2).to_broadcast([128, 8, 64]), ALU.mult)
                    for h in range(8):
                        g = h // 4
                        s.mm(pST[:, h * 64:(h + 1) * 64], Bt[:, a, g * 128:(g + 1) * 128], xw[:, h, :])
                    s.tt(H[:], H[:], dec[:].unsqueeze(2).to_broadcast([128, 8, 64]), ALU.mult)
                    s.tt(H[:], H[:], pST[:].rearrange("p (h d) -> p h d", h=8), ALU.add)
                    s.cp(Hb[:], H[:], eng="act")
            s.flush()

    def phase_merge_a(self, l, ctx_out):
        s = self.s
        with ExitStack() as ph:
            wbr = [s.sb([128, 16, 512], BF16, stack=ph) for _ in range(2)]
            yt = [s.sb([128, 16, 512], BF16, stack=ph) for _ in range(2)]
            gt = [s.sb([128, 4, 512], BF16, stack=ph) for _ in range(6)]
            accf = [s.sb([128, 512], F32, stack=ph) for _ in range(2)]
            tmp = [s.sb([128, 512], F32, stack=ph) for _ in range(3)]
            accT = [s.sb([128, 4, 512], BF16, stack=ph) for _ in range(2)]
            pm = [s.ps([128, 512], F32, stack=ph) for _ in range(6)]
            WBR = self.WBR[l].rearrange("(j p) n -> p j n", p=128)
            GTv = self.GT.rearrange("(b c p) t -> p b c t", b=4, p=128)
            nw = ng = npm = nt3 = 0
            for ti, (t0, nt) in enumerate(TILES):
                if ti == 0 and not ctx_out:
                    continue
                y = yt[ti % 2]
                for br in range(4):
                    s.dma(y[:, br * 4:(br + 1) * 4, :nt], self.YT[br][:, t0:t0 + nt].rearrange("(k p) t -> p k t", p=128))
                for c4 in range(4):
                    w = wbr[nw % 2]
                    nw += 1
                    s.dma(w[:], WBR[:, :, c4 * 512:(c4 + 1) * 512])
                    ao = accT[c4 % 2]
                    for cc in range(4):
                        c = c4 * 4 + cc
                        g_ = gt[ng % 6]
                        ng += 1
                        s.dma(g_[:, :, :nt], GTv[:, :, c, t0:t0 + nt])
                        af = accf[c % 2]
                        for br in range(4):
                            p = pm[npm % 6]
                            npm += 1
                            for k in range(4):
                                s.mm(p[:, :nt], w[:, br * 4 + k, cc * 128:(cc + 1) * 128], y[:, br * 4 + k, :nt], start=(k == 0), stop=(k == 3))
                            if br == 0:
                                s.tt(af[:, :nt], p[:, :nt], g_[:, 0, :nt], ALU.mult)
                            else:
                                t_ = tmp[nt3 % 3]
                                nt3 += 1
                                s.tt(t_[:, :nt], p[:, :nt], g_[:, br, :nt], ALU.mult)
                                if br < 3:
                                    s.tt(af[:, :nt], af[:, :nt], t_[:, :nt], ALU.add, eng="pool")
                                else:
                                    s.tt(ao[:, cc, :nt], af[:, :nt], t_[:, :nt], ALU.add, eng="pool")
                    s.dma(self.ACCT[c4 * 512:(c4 + 1) * 512, t0:t0 + nt].rearrange("(j p) t -> p j t", p=128), ao[:, :, :nt], queue=STQ)
            s.flush()

    def _bc(self, tile_, vec):
        self.s.dma(tile_[:], vec.partition_broadcast(128))

    def ln_affine(self, u, junk, st, lng, lnb):
        s = self.s
        s.memset(st[:], 0.0)
        s.act(junk[:], u[:], AF.Identity, accum_out=st[:, 0:1])
        s.act(junk[:], u[:], AF.Square, accum_out=st[:, 1:2])
        s.ts(st[:, 2:3], st[:, 0:1], 1.0 / D, None, op0=ALU.mult)
        s.ts(st[:, 3:4], st[:, 1:2], 1.0 / D, None, op0=ALU.mult)
        s.tt(st[:, 4:5], st[:, 2:3], st[:, 2:3], ALU.mult)
        s.tt(st[:, 5:6], st[:, 3:4], st[:, 4:5], ALU.subtract)
        s.ts(st[:, 6:7], st[:, 5:6], LN_EPS, None, op0=ALU.add)
        s.sqrt(st[:, 6:7], st[:, 6:7])
        s.recip(st[:, 6:7], st[:, 6:7])
        s.stt(st[:, 7:8], st[:, 2:3], -1.0, st[:, 6:7], ALU.mult, ALU.mult)
        s.act(u[:], u[:], AF.Identity, bias=st[:, 7:8], scale=st[:, 6:7])
        s.tt(u[:], u[:], lng[:], ALU.mult)
        s.tt(u[:], u[:], lnb[:], ALU.add, eng="pool")

    def phase_merge_b(self, l, ctx_out, src):
        s = self.s
        with ExitStack() as ph:
            wout = s.sb([128, KC, D], BF16, stack=ph)
            acc = [s.sb([128, KC, 512], BF16, stack=ph) for _ in range(2)]
            xt = [s.sb([128, D], F32, stack=ph) for _ in range(2)]
            ut = [s.sb([128, D], F32, stack=ph) for _ in range(2)]
            g1 = s.sb([128, D], F32, stack=ph)
            lng = s.sb([128, D], F32, stack=ph)
            lnb = s.sb([128, D], F32, stack=ph)
            yb = [s.sb([128, D], BF16, stack=ph) for _ in range(2)]
            junk = s.sb([128, D], BF16, stack=ph)
            st = [s.sb([128, 8], F32, stack=ph) for _ in range(4)]
            tmp8 = s.sb([128, 8, 128], F32, stack=ph)
            h2 = [s.sb([128, KC, 512], BF16, stack=ph) for _ in range(2)]
            tp = [s.ps([128, 1024], BF16, stack=ph) for _ in range(2)]
            pm = [s.ps([128, 512], F32, stack=ph) for _ in range(4)]
            s.dma(wout[:], self.WOUT[l].rearrange("(k p) n -> p k n", p=128))
            self._bc(lng, self.ln1_g[l])
            self._bc(lnb, self.ln1_b[l])
            npm = nx = 0
            cur_m = None
            for ti, (t0, nt) in enumerate(TILES):
                if ti == 0 and not ctx_out:
                    continue
                m = 1 if ti == 0 else 0
                if m != cur_m:
                    self._bc(g1, self.MODR[l, 0, m])
                    cur_m = m
                a_ = acc[ti % 2]
                h_ = h2[ti % 2]
                s.dma(a_[:, :, :nt], self.ACCT[:, t0:t0 + nt].rearrange("(k p) t -> p k t", p=128))
                sc = self.modc[:, l, 48:64, m]
                sh = self.modc[:, l, 32:48, m]
                for a in range(nt // 128):
                    x_, u_, y_ = xt[nx % 2], ut[nx % 2], yb[nx % 2]
                    st1, st2 = st[(2 * nx) % 4], st[(2 * nx + 1) % 4]
                    nx += 1
                    r0 = t0 + a * 128
                    s.dma(x_[:], src[r0:r0 + 128, :])
                    for cb in range(4):
                        p = pm[npm % 4]
                        npm += 1
                        for k in range(KC):
                            s.mm(p[:], a_[:, k, a * 128:(a + 1) * 128], wout[:, k, cb * 512:(cb + 1) * 512], start=(k == 0), stop=(k == KC - 1))
                        s.tt(u_[:, cb * 512:(cb + 1) * 512], p[:], g1[:, cb * 512:(cb + 1) * 512], ALU.mult)
                    s.stt(u_[:], x_[:], ALPHA, u_[:], ALU.mult, ALU.add)
                    self.ln_affine(u_, junk, st1, lng, lnb)
                    s.dma(self.X1[r0:r0 + 128, :], u_[:], queue=STQ)
                    self.ln_rows(u_, y_, junk, st2)
                    self.mod_transpose(y_, h_, a * 128, tp, tmp8, sc, sh)
                s.dma(self.H2T[:, t0:t0 + nt].rearrange("(k p) t -> p k t", p=128), h_[:, :, :nt], queue=STQ)
            s.flush()

    def phase_ffn(self, l, ctx_out, dst, dst_off):
        s = self.s
        with ExitStack() as ph:
            hid = s.sb([128, FC, 512], BF16, stack=ph)
            wdn = [s.sb([128, 11, 512], BF16, stack=ph) for _ in range(2)]
            uall = s.sb([128, 4, D], F32, stack=ph)
            wg = [s.sb([128, KC, 256], BF16, stack=ph) for _ in range(2)]
            wu = [s.sb([128, KC, 256], BF16, stack=ph) for _ in range(2)]
            h2 = s.sb([128, KC, 514], BF16, stack=ph)
            gb = [s.sb([128, 514], F32, stack=ph) for _ in range(2)]
            ac = [s.sb([128, 512], F32, stack=ph) for _ in range(2)]
            sg = [s.sb([128, 512], F32, stack=ph) for _ in range(2)]
            xt = [s.sb([128, D], F32, stack=ph) for _ in range(2)]
            g2 = s.sb([128, D], F32, stack=ph)
            lng = s.sb([128, D], F32, stack=ph)
            lnb = s.sb([128, D], F32, stack=ph)
            junk = s.sb([128, D], BF16, stack=ph)
            st = [s.sb([128, 8], F32, stack=ph) for _ in range(2)]
            cw = s.sb([128, FC, 3], F32, stack=ph)
            cbias = s.sb([128, FC], F32, stack=ph)
            pb = [s.ps([128, 512], F32, stack=ph) for _ in range(8)]
            pg, pu, pd = pb[0:2], pb[2:4], pb[4:8]
            s.dma(cw[:], self.ffn_convw_col[:, l])
            s.dma(cbias[:], self.ffn_convb_col[:, l])
            self._bc(lng, self.ln2_g[l])
            self._bc(lnb, self.ln2_b[l])
            WUP = self.WUP[l].rearrange("(k p) n -> p k n", p=128)
            WDN = self.WDN[l].rearrange("(f p) n -> p f n", p=128)
            nwg = nfc = nwd = nx = 0
            cur_m = None
            for ti, (t0, nt) in enumerate(TILES):
                if ti == 0 and not ctx_out:
                    continue
                m = 1 if ti == 0 else 0
                if m != cur_m:
                    self._bc(g2, self.MODR[l, 1, m])
                    cur_m = m
                seq_lo, seq_hi = (0, NCTX) if ti == 0 else (NCTX, T)
                left_pad = t0 == seq_lo
                right_pad = t0 + nt == seq_hi
                lo = t0 - (0 if left_pad else 1)
                hi = t0 + nt + (0 if right_pad else 1)
                if left_pad:
                    s.memset(h2[:, :, 0:1], 0.0)
                if right_pad:
                    s.memset(h2[:, :, nt + 1:nt + 2], 0.0)
                s.dma(h2[:, :, (1 if left_pad else 0):(1 if left_pad else 0) + hi - lo], self.H2T[:, lo:hi].rearrange("(k p) t -> p k t", p=128))
                for f2 in range(FC // 2):
                    wg_, wu_ = wg[nwg % 2], wu[nwg % 2]
                    nwg += 1
                    s.dma(wg_[:], WUP[:, :, f2 * 256:(f2 + 1) * 256])
                    s.dma(wu_[:], WUP[:, :, D_FF + f2 * 256:D_FF + (f2 + 1) * 256])
                    for ff in range(2):
                        fc = f2 * 2 + ff
                        pg_, pu_ = pg[nfc % 2], pu[nfc % 2]
                        ph_ = pb[4][:, (nfc % 2) * 2:(nfc % 2) * 2 + 2]
                        gb_, ac_, sg_ = gb[nfc % 2], ac[nfc % 2], sg[nfc % 2]
                        nfc += 1
                        for k in range(KC):
                            s.mm(pg_[:, :nt], wg_[:, k, ff * 128:(ff + 1) * 128], h2[:, k, 1:nt + 1], start=(k == 0), stop=(k == KC - 1))
                        for k in range(KC):
                            s.mm(ph_, wg_[:, k, ff * 128:(ff + 1) * 128], h2[:, k, 0:nt + 2:nt + 1], start=(k == 0), stop=(k == KC - 1))
                        for k in range(KC):
                            s.mm(pu_[:, :nt], wu_[:, k, ff * 128:(ff + 1) * 128], h2[:, k, 1:nt + 1], start=(k == 0), stop=(k == KC - 1))
                        s.cp(gb_[:, 1:nt + 1], pg_[:, :nt], eng="act")
                        s.cp(gb_[:, 0:nt + 2:nt + 1], ph_, eng="act")
                        if left_pad:
                            s.memset(gb_[:, 0:1], 0.0, eng="pool")
                        if right_pad:
                            s.memset(gb_[:, nt + 1:nt + 2], 0.0, eng="pool")
                        s.ts(ac_[:, :nt], gb_[:, 0:nt], cw[:, fc, 0:1], None, op0=ALU.mult)
                        s.stt(ac_[:, :nt], gb_[:, 1:nt + 1], cw[:, fc, 1:2], ac_[:, :nt], ALU.mult, ALU.add)
                        s.stt(ac_[:, :nt], gb_[:, 2:nt + 2], cw[:, fc, 2:3], ac_[:, :nt], ALU.mult, ALU.add)
                        s.act(sg_[:, :nt], ac_[:, :nt], AF.Silu, bias=cbias[:, fc:fc + 1])
                        s.tt(hid[:, fc, :nt], sg_[:, :nt], pu_[:, :nt], ALU.mult)
                na = nt // 128
                for cb in range(4):
                    for q4 in range(4):
                        w = wdn[nwd % 2]
                        nwd += 1
                        s.dma(w[:], WDN[:, q4 * 11:(q4 + 1) * 11, cb * 512:(cb + 1) * 512])
                        for a in range(na):
                            for f in range(11):
                                fc = q4 * 11 + f
                                s.mm(pd[a][:], hid[:, fc, a * 128:(a + 1) * 128], w[:, f, :], start=(fc == 0), stop=(fc == FC - 1))
                    for a in range(na):
                        s.tt(uall[:, a, cb * 512:(cb + 1) * 512], pd[a][:], g2[:, cb * 512:(cb + 1) * 512], ALU.mult)
                for a in range(na):
                    x_ = xt[nx % 2]
                    st_ = st[nx % 2]
                    u_ = uall[:, a, :]
                    nx += 1
                    r0 = t0 + a * 128
                    s.dma(x_[:], self.X1[r0:r0 + 128, :])
                    s.stt(u_, x_[:], ALPHA, u_, ALU.mult, ALU.add)
                    self.ln_affine(u_, junk, st_, lng, lnb)
                    if r0 - dst_off >= 0:
                        s.dma(dst[r0 - dst_off:r0 - dst_off + 128, :], u_, queue=STQ)
            s.flush()

    def build(self):
        self.declare()
        if self.only is not None:
            self.phase0(light=True)
            for (name, l, ctx_out) in self.only:
                getattr(self, "phase_" + name)(l, ctx_out)
            return self.finish()
        self.phase0()
        if self.stop_after == "phase0":
            return self.finish()
        for l in self.layers:
            ctx_out = l < DEPTH - 1
            src = self.xin if l == 0 else self.XS
            self.phase_inproj(l, src)
            if self.stop_after == f"inproj{l}":
                return self.finish()
            self.phase_ssd(l, ctx_out)
            self.phase_na(l, ctx_out)
            self.phase_swa(l, ctx_out)
            self.phase_fnet(l, ctx_out)
            if self.stop_after == f"mix{l}":
                return self.finish()
            self.phase_merge_a(l, ctx_out)
            self.phase_merge_b(l, ctx_out, src)
            if self.stop_after == f"merge{l}":
                return self.finish()
            if l < DEPTH - 1:
                self.phase_ffn(l, ctx_out, self.XS, 0)
            else:
                self.phase_ffn(l, ctx_out, self.out, NCTX)
            if self.stop_after == f"ffn{l}":
                return self.finish()
        return self.finish()

    def finish(self):
        self.s.flush(final=True)
        self.stack.close()
        return self.nc


def host_inputs(inputs):
    f32 = np.float32
    shared = {}
    shared["w_ada"] = np.ascontiguousarray(inputs["w_ada"], dtype=f32)
    shared["b_ada"] = np.ascontiguousarray(inputs["b_ada"], dtype=f32)
    shared["b_ada_col"] = np.ascontiguousarray(inputs["b_ada"].reshape(DEPTH, 96, 128).transpose(2, 0, 1), dtype=f32)
    shared["w_in_r"] = np.ascontiguousarray(inputs["w_in"][:, :, W_PERM], dtype=f32)
    shared["b_gate_col"] = np.ascontiguousarray(inputs["b_gate"].reshape(DEPTH, 64, 128).transpose(2, 0, 1), dtype=f32)
    shared["ident"] = np.eye(128, dtype=f32)
    t = np.arange(NLAT)
    rows = (t // GRID_W).astype(f32)
    cols = (t % GRID_W).astype(f32)
    nf = 16
    inv = (10000.0 ** (-np.arange(nf, dtype=f32) / nf)).astype(f32)
    ang = np.stack([rows[:, None] * inv, cols[:, None] * inv], axis=1).astype(f32)
    cosv, sinv = np.cos(ang).astype(f32), np.sin(ang).astype(f32)
    rc = np.zeros((128, NLAT), f32)
    rs = np.zeros((128, NLAT), f32)
    for r in range(128):
        d = r % 64
        ax, half, f = d // 32, (d % 32) // 16, d % 16
        rc[r] = cosv[:, ax, f]
        rs[r] = sinv[:, ax, f] * (-1.0 if half == 0 else 1.0)
    shared["rope_cos"] = rc
    shared["rope_sin"] = rs
    bf = ml_dtypes.bfloat16
    shared["w_branch"] = np.ascontiguousarray(inputs["w_branch"].reshape(DEPTH, 4 * 512, D), dtype=f32)
    shared["w_out"] = np.ascontiguousarray(inputs["w_out"], dtype=f32)
    shared["ffn_w_up"] = np.ascontiguousarray(inputs["ffn_w_up"], dtype=f32)
    shared["ffn_w_down"] = np.ascontiguousarray(inputs["ffn_w_down"], dtype=f32)
    for k in ("ln1_g", "ln1_b", "ln2_g", "ln2_b"):
        shared[k] = np.ascontiguousarray(inputs[k], dtype=f32)
    shared["ffn_convw_col"] = np.ascontiguousarray(inputs["ffn_conv_w"].reshape(DEPTH, 3, FC, 128).transpose(3, 0, 2, 1), dtype=f32)
    shared["ffn_convb_col"] = np.ascontiguousarray(inputs["ffn_conv_b"].reshape(DEPTH, FC, 128).transpose(2, 0, 1), dtype=f32)
    shared["ssd_convw_col"] = np.ascontiguousarray(inputs["ssd_conv_w"].reshape(DEPTH, 5, 8, 128).transpose(3, 0, 2, 1), dtype=f32)
    shared["ssd_convb_col"] = np.ascontiguousarray(inputs["ssd_conv_b"].reshape(DEPTH, 8, 128).transpose(2, 0, 1), dtype=f32)
    shared["ssd_dt_bias"] = np.ascontiguousarray(inputs["ssd_dt_bias"].reshape(DEPTH, 16), dtype=f32)
    shared["ssd_a_log"] = np.ascontiguousarray(inputs["ssd_a_log"].reshape(DEPTH, 16), dtype=f32)
    shared["ssd_d"] = np.ascontiguousarray(inputs["ssd_d"], dtype=f32)
    shared["ssd_norm_g"] = np.ascontiguousarray(inputs["ssd_norm_g"], dtype=f32)
    jj, ii = np.arange(128)[:, None], np.arange(128)[None, :]
    shared["tri_f"] = (jj <= ii).astype(f32)
    shared["tri_b"] = (jj >= ii).astype(f32)
    k128 = np.arange(128, dtype=np.float64)
    a128 = 2 * np.pi * np.outer(k128, k128) / 128
    shared["cs128"] = (np.concatenate([np.cos(a128), np.sin(a128)], axis=1) / np.sqrt(128.0)).astype(bf)
    kN = np.arange(NLAT, dtype=np.int64)
    aN = 2 * np.pi * ((np.outer(kN, kN) % NLAT).astype(np.float64)) / NLAT
    shared["dftc"] = (np.cos(aN) / np.sqrt(float(NLAT))).astype(bf)
    shared["dfts"] = (-np.sin(aN) / np.sqrt(float(NLAT))).astype(bf)
    del aN
    kC = np.arange(NCTX, dtype=np.float64)
    aC = 2 * np.pi * np.outer(kC, kC) / NCTX
    shared["c256"] = (np.cos(aC) / np.sqrt(float(NCTX))).astype(bf)
    shared["s256"] = (-np.sin(aC) / np.sqrt(float(NCTX))).astype(bf)
    kk, qq = np.arange(128)[:, None], np.arange(128)[None, :]
    shared["band_masks"] = np.ascontiguousarray(np.stack([(qq <= kk), (kk <= qq)], axis=1).astype(np.float32)).astype(bf)
    shared["swa_sink"] = np.ascontiguousarray(inputs["swa_sink"], dtype=f32)
    shared["na_masks"] = np.ascontiguousarray(np.stack([b[1] for b in _NA_BLOCKS], axis=1).astype(np.float32)).astype(bf)
    dr, dc = _na_bias_index()
    rpb = np.asarray(inputs["na_rpb"], dtype=f32)
    g = rpb[:, :, dr, dc]
    g = g.reshape(DEPTH, 4, 2, 7, 128, 128).transpose(0, 1, 4, 2, 3, 5)
    shared["na_bias"] = np.ascontiguousarray(g, dtype=f32)
    per_core = []
    for b in range(NCORES):
        d = {}
        d["xin"] = np.ascontiguousarray(np.concatenate([inputs["ctx"][b], inputs["x"][b]], axis=0), dtype=f32)
        c2 = np.stack([inputs["c"][b], inputs["c_ctx"]], axis=-1)
        d["cT"] = np.ascontiguousarray(c2.reshape(KC, 128, 2).transpose(1, 0, 2), dtype=f32)
        per_core.append(d)
    return shared, per_core


ACTIVE = (0, 1, 4, 5)


def kernel(**inputs):
    shared, per_core = host_inputs(inputs)
    prog = Prog()
    nc = prog.build()
    real = [dict(shared, **pc) for pc in per_core]
    zero = {k: np.zeros_like(v) for k, v in real[0].items()}
    in_maps = [zero] * 8
    in_maps = list(in_maps)
    for b, c in enumerate(ACTIVE):
        in_maps[c] = real[b]
    res = run_bass_kernel_spmd(nc, in_maps, core_ids=list(range(8)))
    out = np.stack([np.asarray(res.results[c]["out"]) for c in ACTIVE], axis=0)
    return out.astype(np.float32)
```
